# Optimizing a Trainium2 kernel written in Bass

```python
import math
import jax
import jax.numpy as jnp
from jax import lax
import numpy as np

D_MODEL = 1024
BATCH = 8
SEQ = 2048
DEPTH = 2
DEC_BATCH = 32
DEC_SEQ = 16
PAST_LEN = 4096

CHUNK = 64
N_HEADS = 8
N_KV_HEADS = 2
HEAD_DIM = 64
GQA_GROUP = N_HEADS // N_KV_HEADS
ROT_DIM = HEAD_DIM // 4
ROPE_THETA = 500000.0
WINDOW = 128
WIN_CHUNKS = WINDOW // CHUNK
SSM_HEADS = 16
SSM_HEAD_DIM = 64
SSM_INNER = SSM_HEADS * SSM_HEAD_DIM
SSM_GROUPS = 2
SSM_STATE = 128
SSM_CHUNK = 64
CONV_WIDTH = 4
CONV_DIM = SSM_INNER + 2 * SSM_GROUPS * SSM_STATE
GM_WIDTH = 512
GM_GROUPS = 4
GM_GROUP_DIM = GM_WIDTH // GM_GROUPS
GM_CHUNK = 128
D_FF = 4 * D_MODEL
N_BRANCH = 3
Q_W = N_HEADS * HEAD_DIM
KV_W = N_KV_HEADS * HEAD_DIM
N_IN = Q_W + 2 * KV_W + SSM_INNER + CONV_DIM + SSM_HEADS + 2 * GM_WIDTH + N_BRANCH * D_MODEL
EPS = 1e-6

kernel_name = 'hybrid_streaming_encoder_step'


def rms_norm(x, g):
    xf = x.astype(jnp.float32)
    y = xf * lax.rsqrt(jnp.mean(xf * xf, axis=-1, keepdims=True) + EPS)
    return (y * g.astype(jnp.float32)).astype(x.dtype)


def layer_norm(x, g, b):
    xf = x.astype(jnp.float32)
    xc = xf - jnp.mean(xf, axis=-1, keepdims=True)
    y = xc * lax.rsqrt(jnp.mean(xc * xc, axis=-1, keepdims=True) + EPS)
    return (y * g.astype(jnp.float32) + b.astype(jnp.float32)).astype(x.dtype)


def partial_rope(x, pos):
    half = ROT_DIM // 2
    inv_freq = ROPE_THETA ** (-jnp.arange(half, dtype=jnp.float32) * (2.0 / ROT_DIM))
    ang = pos.astype(jnp.float32)[:, None] * inv_freq[None, :]
    cos = jnp.cos(ang)[None, :, None, :]
    sin = jnp.sin(ang)[None, :, None, :]
    xf = x.astype(jnp.float32)
    x1 = xf[..., :half]
    x2 = xf[..., half:ROT_DIM]
    out = jnp.concatenate([x1 * cos - x2 * sin, x2 * cos + x1 * sin, xf[..., ROT_DIM:]], axis=-1)
    return out.astype(x.dtype)


def split_projection(proj):
    sizes = (Q_W, KV_W, KV_W, SSM_INNER, CONV_DIM, SSM_HEADS, GM_WIDTH, GM_WIDTH, N_BRANCH * D_MODEL)
    idx = [int(i) for i in np.cumsum(np.array(sizes))[:-1]]
    return jnp.split(proj, idx, axis=-1)


def sink_attend(q, k, v, valid, sinks):
    s = jnp.einsum('bcqkgd,bcnkd->bckgqn', q.astype(jnp.float32), k.astype(jnp.float32)) * (HEAD_DIM ** -0.5)
    s = jnp.where(valid[:, None, None, None, :], s, -jnp.inf)
    sink = sinks.astype(jnp.float32).reshape(N_KV_HEADS, GQA_GROUP)[:, :, None]
    m = jnp.maximum(jnp.max(s, axis=-1), sink)
    p = jnp.exp(s - m[..., None])
    denom = jnp.sum(p, axis=-1) + jnp.exp(sink - m)
    o = jnp.einsum('bckgqn,bcnkd->bcqkgd', p, v.astype(jnp.float32))
    o = o / jnp.moveaxis(denom, -1, 2)[..., None]
    return o.astype(q.dtype)


def attn_prompt(q, k, v, sinks):
    bsz, s = q.shape[0], q.shape[1]
    nc = s // CHUNK
    qb = q.reshape(bsz, nc, CHUNK, N_KV_HEADS, GQA_GROUP, HEAD_DIM)

    def band(t):
        tp = jnp.pad(t, ((0, 0), (WINDOW, 0), (0, 0), (0, 0)))
        tp = tp.reshape(bsz, nc + WIN_CHUNKS, CHUNK, N_KV_HEADS, HEAD_DIM)
        return jnp.concatenate([tp[:, j:j + nc] for j in range(WIN_CHUNKS + 1)], axis=2)

    kb = band(k)
    vb = band(v)
    slot_chunk = jnp.repeat(jnp.arange(WIN_CHUNKS + 1), CHUNK)
    valid = (jnp.arange(nc)[:, None] + slot_chunk[None, :] - WIN_CHUNKS) >= 0
    return sink_attend(qb, kb, vb, valid, sinks).reshape(bsz, s, Q_W)


def attn_sample(q, k, v, cache_k, cache_v, sinks):
    bsz, s = q.shape[0], q.shape[1]
    qb = q.reshape(bsz, 1, s, N_KV_HEADS, GQA_GROUP, HEAD_DIM)
    kb = jnp.concatenate([cache_k.astype(k.dtype), k], axis=1)[:, None]
    vb = jnp.concatenate([cache_v.astype(v.dtype), v], axis=1)[:, None]
    valid = jnp.ones((1, kb.shape[2]), dtype=bool)
    return sink_attend(qb, kb, vb, valid, sinks).reshape(bsz, s, Q_W)


def causal_conv(xbc, conv_state, w, b):
    s = xbc.shape[1]
    xp = jnp.concatenate([conv_state.astype(xbc.dtype), xbc], axis=1)
    out = b
    for i in range(CONV_WIDTH):
        out = out + xp[:, i:i + s] * w[i]
    return jax.nn.silu(out), xp[:, xp.shape[1] - (CONV_WIDTH - 1):]


def ssd_scan(x, dt, a, bm, cm, h0):
    bsz, s = x.shape[0], x.shape[1]
    l = min(SSM_CHUNK, s)
    nc = s // l
    hg = SSM_HEADS // SSM_GROUPS
    xdt = (x * dt[..., None]).reshape(bsz, nc, l, SSM_GROUPS, hg, SSM_HEAD_DIM)
    cum = jnp.cumsum((dt * a).reshape(bsz, nc, l, SSM_GROUPS, hg), axis=2)
    bc = bm.reshape(bsz, nc, l, SSM_GROUPS, SSM_STATE)
    cc = cm.reshape(bsz, nc, l, SSM_GROUPS, SSM_STATE)
    causal = jnp.tril(jnp.ones((l, l), dtype=bool))[:, :, None, None]
    diff = cum[:, :, :, None] - cum[:, :, None, :]
    decay = jnp.exp(jnp.where(causal, diff, -jnp.inf))
    cb = jnp.einsum('bctgn,bcsgn->bctsg', cc, bc)
    y_intra = jnp.einsum('bctsg,bctsgk,bcsgkp->bctgkp', cb, decay, xdt)
    decay_end = jnp.exp(cum[:, :, -1:] - cum)
    chunk_state = jnp.einsum('bcsgn,bcsgk,bcsgkp->bcgkpn', bc, decay_end, xdt)
    chunk_decay = jnp.exp(cum[:, :, -1])

    def step(h, inp):
        dec, st = inp
        return dec[..., None, None] * h + st, h

    h_last, h_in = lax.scan(step, h0.reshape(bsz, SSM_GROUPS, hg, SSM_HEAD_DIM, SSM_STATE),
                            (jnp.moveaxis(chunk_decay, 1, 0), jnp.moveaxis(chunk_state, 1, 0)))
    h_in = jnp.moveaxis(h_in, 0, 1)
    y_inter = jnp.einsum('bctgn,bcgkpn->bctgkp', cc, h_in) * jnp.exp(cum)[..., None]
    y = (y_intra + y_inter).reshape(bsz, s, SSM_HEADS, SSM_HEAD_DIM)
    return y, h_last.reshape(bsz, SSM_HEADS, SSM_HEAD_DIM, SSM_STATE)


def mamba_branch(xbc, z, dt_raw, conv_state, ssm_state, p):
    xbc_act, new_conv = causal_conv(xbc, conv_state, p['conv_w'], p['conv_b'])
    bsz, s = xbc.shape[0], xbc.shape[1]
    xs, bm, cm = jnp.split(xbc_act.astype(jnp.float32), [SSM_INNER, SSM_INNER + SSM_GROUPS * SSM_STATE], axis=-1)
    xs = xs.reshape(bsz, s, SSM_HEADS, SSM_HEAD_DIM)
    dt = jax.nn.softplus(dt_raw.astype(jnp.float32) + p['dt_bias'].astype(jnp.float32))
    a = -jnp.exp(p['a_log'].astype(jnp.float32))
    y, h_last = ssd_scan(xs, dt, a,
                         bm.reshape(bsz, s, SSM_GROUPS, SSM_STATE),
                         cm.reshape(bsz, s, SSM_GROUPS, SSM_STATE),
                         ssm_state.astype(jnp.float32))
    y = y + p['d_skip'].astype(jnp.float32)[:, None] * xs
    y = rms_norm(y.reshape(bsz, s, SSM_INNER) * jax.nn.silu(z.astype(jnp.float32)), p['ssm_norm_w'])
    return y.astype(z.dtype), new_conv, h_last.astype(ssm_state.dtype)


def gmlp_branch(u, v, p):
    bsz, s = u.shape[0], u.shape[1]
    l = min(GM_CHUNK, s)
    nc = s // l
    u = jax.nn.gelu(u)
    v = layer_norm(jax.nn.gelu(v), p['gm_ln_g'], p['gm_ln_b'])
    vg = v.reshape(bsz, nc, l, GM_GROUPS, GM_GROUP_DIM)
    w = jnp.where(jnp.tril(jnp.ones((l, l), dtype=bool)), p['gm_w_s'][:, :l, :l], 0)
    mixed = jnp.einsum('gts,bcsgd->bctgd', w, vg) + jnp.transpose(p['gm_b_s'][:, :l])[:, :, None]
    return u * mixed.reshape(bsz, s, GM_WIDTH).astype(u.dtype), v


def trunk_layer(x, c, pos, conv_state, ssm_state, kv_cache, p):
    bsz, s = x.shape[0], x.shape[1]
    mod = (jax.nn.silu(c) @ p['w_ada'] + p['b_ada']).reshape(bsz, 6, D_MODEL)
    shift1, scale1, gate1, shift2, scale2, gate2 = [mod[:, i][:, None, :] for i in range(6)]
    h = rms_norm(x, p['g_mix']) * (1 + scale1) + shift1
    q, k, v, z, xbc, dt_raw, gu, gv, gates = split_projection(h @ p['w_in'])
    q = partial_rope(q.reshape(bsz, s, N_HEADS, HEAD_DIM), pos)
    k = partial_rope(k.reshape(bsz, s, N_KV_HEADS, HEAD_DIM), pos)
    v = v.reshape(bsz, s, N_KV_HEADS, HEAD_DIM)
    if kv_cache is None:
        attn = attn_prompt(q, k, v, p['sinks'])
        keep_k, keep_v = k[:, s - WINDOW:], v[:, s - WINDOW:]
    else:
        attn = attn_sample(q, k, v, kv_cache[0], kv_cache[1], p['sinks'])
        keep_k, keep_v = k, v
    ssm_out, new_conv, h_last = mamba_branch(xbc, z, dt_raw, conv_state, ssm_state, p)
    gm_out, gm_v = gmlp_branch(gu, gv, p)
    g = jax.nn.sigmoid(gates.astype(jnp.float32)).astype(x.dtype).reshape(bsz, s, N_BRANCH, D_MODEL)
    merged = (g[:, :, 0] * (attn @ p['w_attn_o'])
              + g[:, :, 1] * (ssm_out @ p['w_ssm_o'])
              + g[:, :, 2] * (gm_out @ p['w_gm_o']))
    x = x + gate1 * (merged @ p['w_out'])
    h2 = rms_norm(x, p['g_ff']) * (1 + scale2) + shift2
    x = x + gate2 * (jnp.square(jax.nn.relu(h2 @ p['w_ff1'])) @ p['w_ff2'])
    return x, keep_k, keep_v, new_conv, h_last, gm_v


def setup_inputs(seed: int = 0) -> dict:
    key = jax.random.key(seed)
    keys = jax.random.split(key, 40)
    counter = [0]

    def nxt():
        k = keys[counter[0]]
        counter[0] += 1
        return k

    def nrm(shape, scale):
        return jax.random.normal(nxt(), shape, jnp.float32) * scale

    attn_rows = min(WINDOW, PAST_LEN)
    x_prompt = nrm((BATCH, SEQ, D_MODEL), 1.0)
    x_sample = nrm((DEC_BATCH, DEC_SEQ, D_MODEL), 1.0)
    c_prompt = nrm((BATCH, D_MODEL), 1.0)
    c_sample = nrm((DEC_BATCH, D_MODEL), 1.0)
    cache_attn_k = nrm((DEPTH, DEC_BATCH, attn_rows, N_KV_HEADS, HEAD_DIM), 1.0)
    cache_attn_v = nrm((DEPTH, DEC_BATCH, attn_rows, N_KV_HEADS, HEAD_DIM), 1.0)
    state_ssm = nrm((DEPTH, DEC_BATCH, SSM_HEADS, SSM_HEAD_DIM, SSM_STATE), 0.1)
    state_conv = nrm((DEPTH, DEC_BATCH, CONV_WIDTH - 1, CONV_DIM), 1.0)
    dt0 = jnp.exp(jax.random.uniform(nxt(), (DEPTH, SSM_HEADS), jnp.float32, math.log(1e-3), math.log(1e-1)))
    a0 = jax.random.uniform(nxt(), (DEPTH, SSM_HEADS), jnp.float32, 1.0, 16.0)
    return {
        'x_prompt': x_prompt,
        'x_sample': x_sample,
        'c_prompt': c_prompt,
        'c_sample': c_sample,
        'cache_attn_k': cache_attn_k,
        'cache_attn_v': cache_attn_v,
        'state_ssm': state_ssm,
        'state_conv': state_conv,
        'w_ada': nrm((DEPTH, D_MODEL, 6 * D_MODEL), 0.5 * D_MODEL ** -0.5),
        'b_ada': nrm((DEPTH, 6 * D_MODEL), 0.01),
        'g_mix': 1.0 + nrm((DEPTH, D_MODEL), 0.05),
        'w_in': nrm((DEPTH, D_MODEL, N_IN), D_MODEL ** -0.5),
        'sinks': nrm((DEPTH, N_HEADS), 0.5),
        'conv_w': nrm((DEPTH, CONV_WIDTH, CONV_DIM), CONV_WIDTH ** -0.5),
        'conv_b': nrm((DEPTH, CONV_DIM), 0.01),
        'dt_bias': dt0 + jnp.log(-jnp.expm1(-dt0)),
        'a_log': jnp.log(a0),
        'd_skip': 1.0 + nrm((DEPTH, SSM_HEADS), 0.1),
        'ssm_norm_w': 1.0 + nrm((DEPTH, SSM_INNER), 0.05),
        'gm_ln_g': 1.0 + nrm((DEPTH, GM_WIDTH), 0.05),
        'gm_ln_b': nrm((DEPTH, GM_WIDTH), 0.01),
        'gm_w_s': nrm((DEPTH, GM_GROUPS, GM_CHUNK, GM_CHUNK), GM_CHUNK ** -0.5),
        'gm_b_s': 1.0 + nrm((DEPTH, GM_GROUPS, GM_CHUNK), 0.1),
        'w_attn_o': nrm((DEPTH, Q_W, D_MODEL), Q_W ** -0.5),
        'w_ssm_o': nrm((DEPTH, SSM_INNER, D_MODEL), SSM_INNER ** -0.5),
        'w_gm_o': nrm((DEPTH, GM_WIDTH, D_MODEL), GM_WIDTH ** -0.5),
        'w_out': nrm((DEPTH, D_MODEL, D_MODEL), D_MODEL ** -0.5),
        'g_ff': 1.0 + nrm((DEPTH, D_MODEL), 0.05),
        'w_ff1': nrm((DEPTH, D_MODEL, D_FF), D_MODEL ** -0.5),
        'w_ff2': nrm((DEPTH, D_FF, D_MODEL), D_FF ** -0.5),
        'g_final': 1.0 + nrm((D_MODEL,), 0.05),
    }


def reference(x_prompt, x_sample, c_prompt, c_sample, cache_attn_k, cache_attn_v, state_ssm, state_conv,
              w_ada, b_ada, g_mix, w_in, sinks, conv_w, conv_b, dt_bias, a_log, d_skip, ssm_norm_w,
              gm_ln_g, gm_ln_b, gm_w_s, gm_b_s, w_attn_o, w_ssm_o, w_gm_o, w_out, g_ff, w_ff1, w_ff2,
              g_final):
    bp, sp = x_prompt.shape[0], x_prompt.shape[1]
    pos_p = jnp.arange(sp)
    pos_s = PAST_LEN + jnp.arange(x_sample.shape[1])
    xp = x_prompt
    xs = x_sample
    kp_l, vp_l, sp_l, cp_l = [], [], [], []
    ks_l, vs_l, ss_l, cs_l, gs_l = [], [], [], [], []
    for l in range(DEPTH):
        p = {
            'w_ada': w_ada[l], 'b_ada': b_ada[l], 'g_mix': g_mix[l], 'w_in': w_in[l], 'sinks': sinks[l],
            'conv_w': conv_w[l], 'conv_b': conv_b[l], 'dt_bias': dt_bias[l], 'a_log': a_log[l],
            'd_skip': d_skip[l], 'ssm_norm_w': ssm_norm_w[l], 'gm_ln_g': gm_ln_g[l], 'gm_ln_b': gm_ln_b[l],
            'gm_w_s': gm_w_s[l], 'gm_b_s': gm_b_s[l], 'w_attn_o': w_attn_o[l], 'w_ssm_o': w_ssm_o[l],
            'w_gm_o': w_gm_o[l], 'w_out': w_out[l], 'g_ff': g_ff[l], 'w_ff1': w_ff1[l], 'w_ff2': w_ff2[l],
        }
        zero_conv = jnp.zeros((bp, CONV_WIDTH - 1, CONV_DIM), x_prompt.dtype)
        zero_ssm = jnp.zeros((bp, SSM_HEADS, SSM_HEAD_DIM, SSM_STATE), jnp.float32)
        xp, kp, vp, convp, ssmp, _ = trunk_layer(xp, c_prompt, pos_p, zero_conv, zero_ssm, None, p)
        xs, ksn, vsn, convs, ssms, gvs = trunk_layer(xs, c_sample, pos_s, state_conv[l], state_ssm[l],
                                                     (cache_attn_k[l], cache_attn_v[l]), p)
        kp_l.append(kp)
        vp_l.append(vp)
        sp_l.append(ssmp)
        cp_l.append(convp)
        ks_l.append(ksn)
        vs_l.append(vsn)
        ss_l.append(ssms)
        cs_l.append(convs)
        gs_l.append(gvs)
    y_prompt = rms_norm(xp, g_final)
    y_sample = rms_norm(xs, g_final)
    return (y_prompt, y_sample,
            jnp.stack(kp_l), jnp.stack(vp_l), jnp.stack(sp_l), jnp.stack(cp_l),
            jnp.stack(ks_l), jnp.stack(vs_l), jnp.stack(ss_l), jnp.stack(cs_l), jnp.stack(gs_l))
```

```python
import math
from contextlib import ExitStack

import numpy as np
import concourse.bass as bass
import concourse.mybir as mybir
from concourse.bass_utils import run_bass_kernel_spmd

F32 = mybir.dt.float32
BF16 = mybir.dt.bfloat16
ALU = mybir.AluOpType
AF = mybir.ActivationFunctionType

ENGS = ("pe", "dve", "act", "pool", "sp")


class Res:
    __slots__ = ("name", "writers", "readers")

    def __init__(self, name=""):
        self.name = name
        self.writers = []
        self.readers = []


class DmaSlot:
    def __init__(self, sem):
        self.sem = sem
        self.count = 0


class Op:
    __slots__ = ("eng", "fn", "deps", "slot", "signal", "count", "dma_deps")

    def __init__(self, eng, fn):
        self.eng = eng
        self.fn = fn
        self.deps = []
        self.slot = None
        self.signal = False
        self.count = 0
        self.dma_deps = []


class Prog:
    def __init__(self):
        self.ops = []
        self.last = {}
        self.bar_slots = []
        self.pool_fence = []

    def _dep(self, op, d):
        if d is op:
            return
        if d.slot is not None:
            op.dma_deps.append((d.slot, d.slot.count * 16))
        elif d.fn is not None:
            op.deps.append(d)

    def _track(self, op, reads, writes, deps):
        for r in reads:
            for d in r.writers:
                self._dep(op, d)
            r.readers.append(op)
        for r in writes:
            if r.readers:
                for d in r.readers:
                    self._dep(op, d)
                for d in r.writers:
                    self._dep(op, d)
                r.writers = [op]
                r.readers = []
            else:
                if r.writers and not (r.writers[-1].eng == "pe" and op.eng == "pe" and r.writers[-1].slot is None):
                    self._dep(op, r.writers[-1])
                r.writers.append(op)
                if len(r.writers) > 64:
                    r.writers = r.writers[-64:]
        for d in deps:
            self._dep(op, d)

    def add(self, eng, fn, reads=(), writes=(), deps=()):
        op = Op(eng, fn)
        self._track(op, reads, writes, deps)
        self.ops.append(op)
        if fn is not None:
            self.last[eng] = op
        return op

    def dma(self, queue, slot, out, in_, reads=(), writes=(), deps=()):
        def fn(e, out=out, in_=in_):
            return e.dma_start(out=out, in_=in_)
        op = Op(queue, fn)
        self._track(op, reads, writes, deps)
        slot.count += 1
        op.slot = slot
        self.ops.append(op)
        return op

    def barrier(self, engs=("pe", "dve", "act", "sp")):
        last = dict(self.last)
        self.pool_fence = [o for k, o in last.items() if k in ("pe", "dve", "act")]
        for e in engs:
            ds = [o for k, o in last.items() if k != e and k in ("pe", "dve", "act", "pool")]
            op = self.add(e, None, deps=ds)
            for sl in self.bar_slots:
                if sl.count:
                    op.dma_deps.append((sl, sl.count * 16))

    def emit(self, sems, block):
        for i, op in enumerate(self.ops):
            op.count = i
        for op in self.ops:
            best = {}
            for d in op.deps:
                if d.eng not in best or d.count > best[d.eng].count:
                    best[d.eng] = d
            op.deps = list(best.values())
            for d in op.deps:
                d.signal = True
        for op in self.ops:
            op.count = 0
        cnt = {e: 0 for e in ENGS}
        per = {e: [] for e in ENGS}
        for op in self.ops:
            if op.slot is None and op.signal and op.fn is not None:
                cnt[op.eng] += 1
                op.count = cnt[op.eng]
            per[op.eng].append(op)
        engobj = {"pe": "tensor", "dve": "vector", "act": "scalar", "pool": "gpsimd", "sp": "sync"}

        def make(ename):
            oplist = per[ename]

            def body(e):
                waited = {}
                waited_dma = {}
                for op in oplist:
                    need = {}
                    for d in op.deps:
                        if d.count > need.get(d.eng, 0):
                            need[d.eng] = d.count
                    for pe_, c in need.items():
                        if pe_ == ename and ename == "pe":
                            continue
                        if waited.get(pe_, 0) < c:
                            e.wait_ge(sems[pe_], c)
                            waited[pe_] = c
                    for slot, v in op.dma_deps:
                        if waited_dma.get(id(slot), 0) < v:
                            e.wait_ge(slot.sem, v)
                            waited_dma[id(slot)] = v
                    if op.fn is None:
                        continue
                    ins = op.fn(e)
                    if op.slot is not None:
                        ins.then_inc(op.slot.sem, 16)
                    elif op.signal:
                        ins.then_inc(sems[ename], 1)
            return body

        for ename in ENGS:
            if per[ename]:
                getattr(block, engobj[ename])(make(ename))


D = 1024
TP = 1024
TS = 32
T = TP + TS
NT = [(0, 352), (352, 352), (704, 352)]
NB = 6
EPS = 1e-6
NWCH = 404
AW = 16896

O_Q, O_K, O_V, O_Z, O_X, O_B, O_C, O_DT, O_GU, O_GV, O_G = 0, 512, 640, 768, 1792, 2816, 3072, 3328, 3344, 3856, 4368

LC = {}
_o = 0
for _n, _w in (("gmix", 8), ("gff", 8), ("convw", 48), ("convb", 12), ("normw", 8), ("dskip", 8), ("sink", 4),
               ("bada", 48), ("dtb", 16), ("alog", 16)):
    LC[_n] = (_o, _w)
    _o += _w
LCW = _o
GC = {}
_o = 2 * LCW
for _n, _w in (("gfin", 8), ("m01", 128), ("su", 128), ("onesf", 128), ("m01s", 32), ("mk", 2), ("m01g", 128), ("mkp", 2)):
    GC[_n] = (_o, _w)
    _o += _w
NCF = _o
BC = {}
_o = 0
for _n, _w in (("ident", 128), ("ones", 128), ("ones_lo", 128), ("ones_hi", 128), ("su", 128), ("m01", 128), ("m01s", 32)):
    BC[_n] = (_o, _w)
    _o += _w
NCB = _o
GT_W = 512 + 512 + 512 + 128
GW_W = 4 * 128 + 4 * 32


def _rot_src(d):
    if d < 8:
        return d + 8
    if d < 16:
        return d - 8
    return -1


class Builder:
    def __init__(self, limit=100000):
        self.keys = []
        self.keyidx = {}
        self.limit = limit
        import os
        self.sub = int(os.environ.get("KSUB", "100"))
        self.ss = int(os.environ.get("KSS", "100"))

    def wkey(self, key):
        if key not in self.keyidx:
            self.keyidx[key] = len(self.keys)
            self.keys.append(key)
        return self.keyidx[key]

    def build(self, nw_chunks):
        nc = bass.Bass("TRN2", target_bir_lowering=False)
        self.nc = nc
        P = Prog()
        self.P = P

        def din(name, shape):
            return nc.dram_tensor(name, shape, F32, kind="ExternalInput").ap()

        def dout(name, shape):
            return nc.dram_tensor(name, shape, F32, kind="ExternalOutput").ap()

        xT_d = din("xT", [2, 128, 8, T])
        cT_d = din("cT", [128, 8, 5])
        wst_d = din("wst", [nw_chunks, 128, 8, 128])
        cf_d = din("cf", [128, NCF])
        cb_d = din("cb", [128, NCB])
        rope_d = din("rope", [2, 2, 128, T])
        gt_d = din("gt", [2, 128, GT_W])
        gw_d = din("gw", [2, 128, GW_W])
        kc_d = din("kc", [4, 2, 128, 2, 128])
        vc_d = din("vc", [4, 2, 128, 2, 2, 128])
        hs_d = din("hs", [4, 2, 128, 1024])
        cs_d = din("cs", [4, 2, 128, 12, 3])

        yT_o = dout("yT", [2, 128, 8, T])
        kp_o = dout("kp", [2, 128, 2, 128])
        vp_o = dout("vp", [2, 128, 128])
        ks_o = dout("ks", [2, 2, 128, 2, 32])
        vs_o = dout("vs", [2, 2, 32, 128])
        hp_o = dout("hp", [2, 128, 1024])
        hso_o = dout("hso", [2, 4, 128, 1024])
        cp_o = dout("cpo", [2, 128, 12, 3])
        cso_o = dout("cso", [2, 4, 128, 12, 3])
        gv_o = dout("gvo", [2, 2, 32, 512])

        with ExitStack() as es:
            E = es.enter_context

            def sb(name, shape, dt):
                return E(nc.sbuf_tensor(name, shape, dt))

            X = [sb("X0", [128, 8, T], F32), sb("X1", [128, 8, T], F32)]
            rX = [[Res() for _ in range(8)] for _ in range(2)]
            H = sb("H", [128, 8, T], BF16)
            rHt = [Res() for _ in NT]

            class _HRes:
                def __call__(self, t0, tn):
                    return [rHt[i] for i, (a, n) in enumerate(NT) if a < t0 + tn and t0 < a + n]
            rHof = _HRes()
            MG = sb("MG", [128, 8, T], BF16)
            rMG = Res()
            BR = sb("BR", [128, 8, T], BF16)
            rBR = Res()
            WB = [sb(f"wb{i}", [128, 8, 128], BF16) for i in range(NB)]
            rWB = [Res() for _ in range(NB)]
            ARENA = sb("arena", [128, AW], F32)
            CF = sb("cf_s", [128, NCF], F32)
            rCF = Res()
            CB = sb("cb_s", [128, NCB], BF16)
            rCB = Res()
            MOD = sb("mod", [128, 2, 48, 5], F32)
            rMOD = Res()
            AA = sb("aa", [128, 2, 2, 8, 5], F32)
            rAA = Res()
            ESK = sb("esk", [128, 2, 4], F32)
            rESK = Res()
            AROW = sb("arow", [128, 2, 16], F32)
            rAROW = Res()
            KTC = sb("ktc", [128, 2, 128], BF16)
            rKTC = Res()
            VLC = sb("vlc", [128, 4, 128], BF16)
            rVLC = Res()
            CCAR = sb("ccar", [128, 12, 3], F32)
            rCCAR = Res()
            HST = sb("hst", [128, 2, 512], F32)
            rHST = [Res(), Res()]
            CTs = sb("cts", [128, 8, 5], F32)
            CTb = sb("ctb", [128, 8, 5], BF16)
            rCT = Res()
            PS = [E(nc.psum_tensor(f"ps{i}", [128, 512], F32)) for i in range(8)]
            rPS = [Res() for _ in range(8)]

            sems = {e: E(nc.semaphore(f"s_{e}")) for e in ("pe", "dve", "act", "pool")}
            wsl = [DmaSlot(E(nc.semaphore(f"w{i}"))) for i in range(NB)]
            ldsl = DmaSlot(E(nc.semaphore("ld")))
            ld2 = DmaSlot(E(nc.semaphore("ld2")))
            osl = DmaSlot(E(nc.semaphore("os")))
            block = E(nc.Block())
            P.bar_slots = [osl]

            def pool_arena_dma(out, in_, Wr):
                return P.dma("pool", ld2, out, in_, writes=Wr, deps=[o for k, o in P.last.items() if k in ("pe", "dve", "act")])

            def cf(name, l=None):
                if l is None:
                    o, w = GC[name]
                else:
                    o, w = LC[name]
                    o += l * LCW
                return CF[:, o:o + w]

            def cb(name):
                o, w = BC[name]
                return CB[:, o:o + w]

            arena_top = [0]

            def aalloc(shape, dt):
                n = int(np.prod(shape))
                nbytes = n * (4 if dt == F32 else 2)
                w = (nbytes + 3) // 4
                w = (w + 7) // 8 * 8
                a = ARENA[:, arena_top[0]:arena_top[0] + w]
                arena_top[0] += w
                assert arena_top[0] <= AW, f"arena overflow {arena_top[0]}"
                v = a if dt == F32 else a.bitcast(dt)
                v = v[:, :n]
                if len(shape) > 1:
                    names = "abcdef"[:len(shape)]
                    pat = f"p ({' '.join(names)}) -> p {' '.join(names)}"
                    v = v.rearrange(pat, **{nm: s for nm, s in zip(names, shape)})
                return v, Res()

            def areset():
                P.barrier()
                arena_top[0] = 0

            wcount = [0]

            def wnext(key):
                idx = self.wkey(key)
                s = wcount[0] % NB
                wcount[0] += 1
                P.dma("pool", wsl[s], WB[s][:], wst_d[idx], writes=[rWB[s]])
                return WB[s], rWB[s]

            def mm(out, lhsT, rhs, start, stop, R, Wr):
                return P.add("pe", lambda e: e.matmul(out, lhsT=lhsT, rhs=rhs, start=start, stop=stop), reads=R, writes=Wr)

            def tp(out, in_, ident, R, Wr):
                return P.add("pe", lambda e: e.transpose(out, in_, ident), reads=R, writes=Wr)

            def act(out, in_, func, R, Wr, bias=None, scale=None):
                kw = {}
                if bias is not None:
                    kw["bias"] = bias
                if scale is not None:
                    kw["scale"] = scale
                return P.add("act", lambda e: e.activation(out=out, in_=in_, func=func, **kw), reads=R, writes=Wr)

            def tt(eng, out, a, b, op, R, Wr):
                return P.add(eng, lambda e: e.tensor_tensor(out=out, in0=a, in1=b, op=op), reads=R, writes=Wr)

            def ts(eng, out, a, s1, s2, op0, op1, R, Wr):
                if op1 is None:
                    return P.add(eng, lambda e: e.tensor_scalar(out=out, in0=a, scalar1=s1, scalar2=None, op0=op0), reads=R, writes=Wr)
                return P.add(eng, lambda e: e.tensor_scalar(out=out, in0=a, scalar1=s1, scalar2=s2, op0=op0, op1=op1), reads=R, writes=Wr)

            def stt(eng, out, a, s, b, op0, op1, R, Wr):
                return P.add(eng, lambda e: e.scalar_tensor_tensor(out=out, in0=a, scalar=s, in1=b, op0=op0, op1=op1), reads=R, writes=Wr)

            def cpy(eng, out, in_, R, Wr):
                if eng == "act":
                    return P.add("act", lambda e: e.activation(out=out, in_=in_, func=AF.Copy), reads=R, writes=Wr)
                return P.add(eng, lambda e: e.tensor_copy(out=out, in_=in_), reads=R, writes=Wr)

            def rsqrt_inplace(ap, r):
                act(ap, ap, AF.Ln, [r], [r])
                act(ap, ap, AF.Exp, [r], [r], scale=-0.5)

            def recip(out, in_, R, Wr):
                return P.add("dve", lambda e: e.reciprocal(out=out, in_=in_), reads=R, writes=Wr)

            def memset(eng, ap, val, Wr):
                return P.add(eng, lambda e: e.memset(ap, val), writes=Wr)

            def odma(out, in_, R):
                return P.dma("sp", osl, out, in_, reads=R)

            def rs_of(rows):
                return slice(0, rows)

            bankrr = [0]

            def nextbank(banks):
                b = banks[bankrr[0] % len(banks)]
                bankrr[0] += 1
                return b

            def groups(b):
                return [(0, TP, 0), (TP, 16, 1 + 2 * b), (TP + 16, 16, 2 + 2 * b)]

            P.dma("sp", ldsl, CF[:], cf_d, writes=[rCF])
            P.dma("pool", ld2, CB[:], cb_d, writes=[rCB])
            P.dma("sp", ldsl, CTs[:], cT_d, writes=[rCT])
            for b in range(2):
                for c in range(8):
                    P.dma("sp", ldsl, X[b][:, c, :], xT_d[b, :, c, :], writes=[rX[b][c]])
            act(CTb[:], CTs[:], AF.Silu, [rCT], [rCT])
            for l in range(2):
                bank = 0
                for c in range(48):
                    wb, rw = wnext(("ada", l, c))
                    for k in range(8):
                        mm(PS[bank][:, c * 5:(c + 1) * 5], wb[:, k, :], CTb[:, k, :], k == 0, k == 7, [rw, rCT], [rPS[bank]])
                bada = cf("bada", l)
                tt("dve", MOD[:, l, :, :], PS[bank][:, 0:240].rearrange("p (c s) -> p c s", c=48),
                   bada.unsqueeze(2).to_broadcast([128, 48, 5]), ALU.add, [rPS[bank], rCF], [rMOD])
                for wi, (gname, mi) in enumerate((("gmix", 1), ("gff", 4))):
                    stt("dve", AA[:, l, wi, :, :], MOD[:, l, mi * 8:(mi + 1) * 8, :], 1.0,
                        cf(gname, l).unsqueeze(2).to_broadcast([128, 8, 5]), ALU.add, ALU.mult, [rMOD, rCF], [rAA])
                act(ESK[:, l, :], cf("sink", l), AF.Exp, [rCF], [rESK])
                act(AROW[:, l, :], cf("alog", l), AF.Exp, [rCF], [rAROW])
                ts("dve", AROW[:, l, :], AROW[:, l, :], -1.0, None, ALU.mult, None, [rAROW], [rAROW])

            def modcol(l, mi, c, s):
                return MOD[:, l, mi * 8 + c, s:s + 1]

            def rms_rstd(b, dst, rdst):
                banks = [0, 1, 2]
                sqs = [aalloc([T], BF16) for _ in range(2)]
                for c in range(8):
                    sq, rsq = sqs[c % 2]
                    act(sq[:, :], X[b][:, c, :], AF.Square, [rX[b][c]], [rsq])
                    for ti, (t0, tn) in enumerate(NT):
                        mm(PS[banks[ti]][:, :tn], cb("ones"), sq[:, t0:t0 + tn], c == 0, c == 7, [rsq, rCB], [rPS[banks[ti]]])
                for ti, (t0, tn) in enumerate(NT):
                    ts("dve", dst[:, t0:t0 + tn], PS[banks[ti]][:, :tn], 1.0 / D, EPS, ALU.mult, ALU.add, [rPS[banks[ti]]], [rdst])
                rsqrt_inplace(dst[:, :], rdst)

            def norm_mod(b, l, wi, mi_shift):
                rstd, _ = aalloc([T], F32)
                rr = [Res() for _ in NT]
                sqs = [aalloc([352], BF16) for _ in range(4)]
                tmps = [aalloc([352], F32) for _ in range(4)]
                banks = [0, 1, 2]
                for ti, (t0, tn) in enumerate(NT):
                    bank = banks[ti]
                    for c in range(8):
                        sq, rsq = sqs[c % 4]
                        act(sq[:, :tn], X[b][:, c, t0:t0 + tn], AF.Square, [rX[b][c]], [rsq])
                        mm(PS[bank][:, :tn], cb("ones"), sq[:, :tn], c == 0, c == 7, [rsq, rCB], [rPS[bank]])
                    ts("dve", rstd[:, t0:t0 + tn], PS[bank][:, :tn], 1.0 / D, EPS, ALU.mult, ALU.add, [rPS[bank]], [rr[ti]])
                    rsqrt_inplace(rstd[:, t0:t0 + tn], rr[ti])
                    for c in range(8):
                        tmp, rtmp = tmps[c % 4]
                        tt("dve", tmp[:, :tn], X[b][:, c, t0:t0 + tn], rstd[:, t0:t0 + tn], ALU.mult, [rX[b][c], rr[ti]], [rtmp])
                        for (g0, gn, sq_) in groups(b):
                            lo, hi = max(g0, t0), min(g0 + gn, t0 + tn)
                            if lo >= hi:
                                continue
                            act(H[:, c, lo:hi], tmp[:, lo - t0:hi - t0], AF.Identity, [rtmp, rAA, rMOD], [rHt[ti]],
                                bias=modcol(l, mi_shift, c, sq_), scale=AA[:, l, wi, c, sq_:sq_ + 1])

            def proj_fm(wb, rw, kn, koff, src, rsrc, banks, evac):
                for ti, (t0, tn) in enumerate(NT):
                    bank = nextbank(banks)
                    rs_list = rsrc(t0, tn) if callable(rsrc) else [rsrc]
                    for k in range(kn):
                        mm(PS[bank][:, :tn], wb[:, koff + k, :], src[:, k, t0:t0 + tn], k == 0, k == kn - 1, [rw] + rs_list, [rPS[bank]])
                    evac(bank, ti, t0, tn)

            def merge_branch(l, bi, kn, packed):
                areset()
                gs = [aalloc([512], F32) for _ in range(2)]
                tms = [aalloc([512], F32) for _ in range(2)]
                cnt = 0
                for c in range(8):
                    if packed:
                        if c % 2 == 0:
                            wo, rwo = wnext((("ao" if bi == 0 else "go"), l, c // 2))
                        koff = (c % 2) * 4
                    else:
                        wo, rwo = wnext(("so", l, c))
                        koff = 0
                    wg, rwg = wnext(("in", l, "gate", bi, c))
                    for ti, (t0, tn) in enumerate(NT):
                        b1 = nextbank([0, 1, 2, 3, 4, 5, 6, 7])
                        b2 = nextbank([0, 1, 2, 3, 4, 5, 6, 7])
                        for k in range(8):
                            mm(PS[b1][:, :tn], wg[:, k, :], H[:, k, t0:t0 + tn], k == 0, k == 7, [rwg] + rHof(t0, tn), [rPS[b1]])
                        for k in range(kn):
                            mm(PS[b2][:, :tn], wo[:, koff + k, :], BR[:, k, t0:t0 + tn], k == 0, k == kn - 1, [rwo, rBR], [rPS[b2]])
                        g, rg = gs[cnt % 2]
                        tm, rtm = tms[cnt % 2]
                        cnt += 1
                        act(g[:, :tn], PS[b1][:, :tn], AF.Sigmoid, [rPS[b1]], [rg])
                        if bi == 0:
                            tt("dve", MG[:, c, t0:t0 + tn], PS[b2][:, :tn], g[:, :tn], ALU.mult, [rPS[b2], rg], [rMG])
                        else:
                            tt("dve", tm[:, :tn], PS[b2][:, :tn], g[:, :tn], ALU.mult, [rPS[b2], rg], [rtm])
                            tt("dve", MG[:, c, t0:t0 + tn], MG[:, c, t0:t0 + tn], tm[:, :tn], ALU.add, [rtm, rMG], [rMG])

            def attention(b, l):
                areset()
                QT, rQT = aalloc([4, 2, T], BF16)
                KT, rKT = aalloc([2, 128 + T], BF16)
                VLH, rVLH = aalloc([10, 4, 128], BF16)
                COS, rCOS = aalloc([T], F32)
                SIN, rSIN = aalloc([T], F32)
                KF, rKF = aalloc([2, 128 + TS], F32)
                VF, rVF = aalloc([2, 128], F32)
                t1s = [aalloc([512], F32) for _ in range(2)]
                t2s = [aalloc([512], F32) for _ in range(2)]
                PTp = [aalloc([4, 128], BF16) for _ in range(2)]
                PTc = [aalloc([4, 128], BF16) for _ in range(2)]
                RD = [aalloc([128], F32) for _ in range(2)]
                KCs, rKCs = aalloc([2, 2, 128], BF16)
                VCs, rVCs = aalloc([2, 2, 2, 128], BF16)
                PTsc, rPTsc = aalloc([4, 16], BF16)
                PTsn, rPTsn = aalloc([4, 16], BF16)
                P.dma("sp", ldsl, COS[:, :], rope_d[b, 0], writes=[rCOS])
                P.dma("sp", ldsl, SIN[:, :], rope_d[b, 1], writes=[rSIN])
                for s in range(2):
                    pool_arena_dma(KCs[:, s, :, :], kc_d[2 * b + s, l], [rKCs])
                    pool_arena_dma(VCs[:, s, :, :, :], vc_d[2 * b + s, l], [rVCs])
                memset("dve", VLH[:, :, :, :], 0.0, [rVLH])
                memset("dve", QT[:, :, :, :], 0.0, [rQT])
                for i in range(2):
                    memset("dve", PTp[i][0][:, :, :], 0.0, [PTp[i][1]])
                    memset("dve", PTc[i][0][:, :, :], 0.0, [PTc[i][1]])
                if b == 1:
                    cpy("dve", KT[:, :, 0:128], KTC[:, :, :], [rKTC], [rKT])
                    cpy("dve", VLH[:, 0, :, :], VLC[:, :, :], [rVLC], [rVLH])
                else:
                    memset("dve", KT[:, :, 0:128], 0.0, [rKT])

                if self.sub < 1:
                    return
                def rope_proj(kind, kindr, j, dst_fn):
                    wa, rwa = wnext(("in", l, kind, j))
                    wr, rwr = wnext(("in", l, kindr, j))
                    for ti, (t0, tn) in enumerate(NT):
                        b1 = nextbank([0, 1, 2, 3, 4, 5, 6, 7])
                        b2 = nextbank([0, 1, 2, 3, 4, 5, 6, 7])
                        for k in range(8):
                            mm(PS[b1][:, :tn], wa[:, k, :], H[:, k, t0:t0 + tn], k == 0, k == 7, [rwa] + rHof(t0, tn), [rPS[b1]])
                        for k in range(8):
                            mm(PS[b2][:, :tn], wr[:, k, :], H[:, k, t0:t0 + tn], k == 0, k == 7, [rwr] + rHof(t0, tn), [rPS[b2]])
                        t1, rt1 = t1s[ti % 2]
                        t2, rt2 = t2s[ti % 2]
                        tt("dve", t1[:, :tn], PS[b1][:, :tn], COS[:, t0:t0 + tn], ALU.mult, [rPS[b1], rCOS], [rt1])
                        tt("dve", t2[:, :tn], PS[b2][:, :tn], SIN[:, t0:t0 + tn], ALU.mult, [rPS[b2], rSIN], [rt2])
                        dst_fn(ti, t0, tn, t1, rt1, t2, rt2)

                for j in range(4):
                    def dq(ti, t0, tn, t1, rt1, t2, rt2, j=j):
                        tt("dve", QT[0:64, j, 0, t0:t0 + tn], t1[0:64, :tn], t2[0:64, :tn], ALU.add, [rt1, rt2], [rQT])
                        tt("dve", QT[64:128, j, 1, t0:t0 + tn], t1[64:128, :tn], t2[64:128, :tn], ALU.add, [rt1, rt2], [rQT])
                    rope_proj("q", "qr", j, dq)
                for g in range(2):
                    def dk(ti, t0, tn, t1, rt1, t2, rt2, g=g):
                        tt("dve", KT[:, g, 128 + t0:128 + t0 + tn], t1[:, :tn], t2[:, :tn], ALU.add, [rt1, rt2], [rKT])
                        if b == 1:
                            lo, hi = max(t0, TP - 128), min(t0 + tn, TP)
                            if lo < hi:
                                tt("dve", KF[:, g, lo - (TP - 128):hi - (TP - 128)], t1[:, lo - t0:hi - t0], t2[:, lo - t0:hi - t0], ALU.add,
                                   [rt1, rt2], [rKF])
                        lo, hi = max(t0, TP), min(t0 + tn, T)
                        if lo < hi:
                            tt("dve", KF[:, g, 128 + lo - TP:128 + hi - TP], t1[:, lo - t0:hi - t0], t2[:, lo - t0:hi - t0], ALU.add,
                               [rt1, rt2], [rKF])
                    rope_proj("k", "kr", g, dk)
                if self.sub < 2:
                    return
                wv, rwv = wnext(("in", l, "v"))
                for i in range(9):
                    rows = 128 if i < 8 else TS
                    t0 = i * 128
                    bank = nextbank([0, 1, 2, 3, 4, 5, 6, 7])
                    for k in range(8):
                        mm(PS[bank][:rows, 0:128], H[:, k, t0:t0 + rows], wv[:, k, :], k == 0, k == 7, [rwv] + rHof(t0, rows), [rPS[bank]])
                    for g in range(2):
                        cpy("act", VLH[:rows, i + 1, 2 * g, 0:64], PS[bank][:rows, g * 64:(g + 1) * 64], [rPS[bank]], [rVLH])
                        cpy("act", VLH[:rows, i + 1, 2 * g + 1, 64:128], PS[bank][:rows, g * 64:(g + 1) * 64], [rPS[bank]], [rVLH])
                    if i == 7 and b == 1:
                        cpy("dve", VF[:, 0, :], PS[bank][:, 0:128], [rPS[bank]], [rVF])
                    if i == 8:
                        cpy("dve", VF[:rows, 1, :], PS[bank][:rows, 0:128], [rPS[bank]], [rVF])
                if self.sub < 3:
                    return
                if b == 1:
                    odma(kp_o[l], KF[:, :, 0:128], [rKF])
                    odma(vp_o[l], VF[:, 0, :], [rVF])
                else:
                    cpy("dve", KTC[:, :, :], KT[:, :, 128 + 896:128 + 1024], [rKT], [rKTC])
                    cpy("dve", VLC[:, :, :], VLH[:, 8, :, :], [rVLH], [rVLC])
                odma(ks_o[l, b], KF[:, :, 128:128 + TS], [rKF])
                odma(vs_o[l, b], VF[:TS, 1, :], [rVF])

                if self.sub < 4:
                    return
                ptstore = {}

                def stageQ(i, g, it):
                    have_prev = not (b == 0 and i == 0)
                    qc = slice(i * 128, (i + 1) * 128)
                    pts = {}
                    for kt in ((0, 1) if have_prev else (1,)):
                        bank = nextbank([0, 1, 2, 3])
                        kcols = slice(i * 128, (i + 1) * 128) if kt == 0 else slice(128 + i * 128, 128 + (i + 1) * 128)
                        for hh in range(4):
                            jj = 2 * g + hh // 2
                            half = hh % 2
                            mm(PS[bank][:, hh * 128:(hh + 1) * 128], KT[:, g, kcols], QT[:, jj, half, qc], True, True,
                               [rKT, rQT], [rPS[bank]])
                        psv = PS[bank][:, :].rearrange("p (h q) -> p h q", h=4)
                        if kt == 0:
                            pt, rpt = PTp[it % 2]
                            act(pt[:, :, 0:64], psv[:, :, 0:64], AF.Exp, [rPS[bank]], [rpt], scale=0.125)
                            act(pt[64:128, :, 64:128], psv[64:128, :, 64:128], AF.Exp, [rPS[bank]], [rpt], scale=0.125)
                        else:
                            pt, rpt = PTc[it % 2]
                            act(pt[0:64, :, 0:64], psv[0:64, :, 0:64], AF.Exp, [rPS[bank]], [rpt], scale=0.125)
                            act(pt[:, :, 64:128], psv[:, :, 64:128], AF.Exp, [rPS[bank]], [rpt], scale=0.125)
                        pts[(g, kt)] = (pt, rpt)
                    ptstore[(i, g)] = pts

                def stageP(i, g):
                    have_prev = not (b == 0 and i == 0)
                    qc = slice(i * 128, (i + 1) * 128)
                    pts = ptstore.pop((i, g))
                    for jl in range(2):
                        jj = 2 * g + jl
                        bo = nextbank([4, 5, 6, 7])
                        kts = (0, 1) if have_prev else (1,)
                        n_mm = 2 * len(kts)
                        for which in range(2):
                            col = slice(which * 128, (which + 1) * 128)
                            m = 0
                            for kt in kts:
                                pt, rpt = pts[(g, kt)]
                                vt = i if kt == 0 else i + 1
                                for half in range(2):
                                    hh = 2 * jl + half
                                    if which == 0:
                                        lh = VLH[:, vt, 2 * g + half, :]
                                    else:
                                        lh = cb("ones_lo") if half == 0 else cb("ones_hi")
                                    mm(PS[bo][:, col], lh, pt[:, hh, :], m == 0, m == n_mm - 1, [rVLH, rpt, rCB], [rPS[bo]])
                                    m += 1
                        rd, rrd = RD[(i * 4 + jj) % 2]
                        ts("dve", rd[:, :], PS[bo][:, 128:256], ESK[:, l, jj:jj + 1], None, ALU.add, None, [rPS[bo], rESK], [rrd])
                        recip(rd[:, :], rd[:, :], [rrd], [rrd])
                        tt("dve", BR[:, jj, qc], PS[bo][:, 0:128], rd[:, :], ALU.mult, [rPS[bo], rrd], [rBR])

                seq = [(i, g) for i in range(8) for g in range(2)]
                stageQ(seq[0][0], seq[0][1], 0)
                for n_, (i, g) in enumerate(seq):
                    if n_ + 1 < len(seq):
                        stageQ(seq[n_ + 1][0], seq[n_ + 1][1], n_ + 1)
                    stageP(i, g)
                if self.sub < 5:
                    return
                mk = cf("mk")
                for s in range(2):
                    qc = slice(TP + 16 * s, TP + 16 * (s + 1))
                    for g in range(2):
                        bc = nextbank([0, 1, 2, 3])
                        bn = nextbank([0, 1, 2, 3])
                        for hh in range(4):
                            jj = 2 * g + hh // 2
                            half = hh % 2
                            mm(PS[bc][:, hh * 16:(hh + 1) * 16], KCs[:, s, g, :], QT[:, jj, half, qc], True, True, [rKCs, rQT], [rPS[bc]])
                            mm(PS[bn][:TS, hh * 16:(hh + 1) * 16], KT[:, g, 128 + TP:128 + T], QT[:, jj, half, qc], True, True, [rKT, rQT], [rPS[bn]])
                        act(PTsc[:, :, :], PS[bc][:, 0:64].rearrange("p (h q) -> p h q", h=4), AF.Exp, [rPS[bc]], [rPTsc], scale=0.125)
                        act(PTsn[:TS, :, :], PS[bn][:TS, 0:64].rearrange("p (h q) -> p h q", h=4), AF.Exp, [rPS[bn]], [rPTsn], scale=0.125)
                        ts("dve", PTsn[:TS, :, :], PTsn[:TS, :, :], mk[:TS, s:s + 1], None, ALU.mult, None, [rPTsn, rCF], [rPTsn])
                        for jl in range(2):
                            jj = 2 * g + jl
                            bo = nextbank([4, 5, 6, 7])
                            for which in range(2):
                                col = slice(which * 16, (which + 1) * 16)
                                m = 0
                                for src in range(2):
                                    for half in range(2):
                                        hh = 2 * jl + half
                                        if src == 0:
                                            lh = VCs[:, s, g, half, :] if which == 0 else (cb("ones_lo") if half == 0 else cb("ones_hi"))
                                            rhs = PTsc[:, hh, :]
                                        else:
                                            lh = VLH[:TS, 9, 2 * g + half, :] if which == 0 else (cb("ones_lo")[:TS, :] if half == 0 else cb("ones_hi")[:TS, :])
                                            rhs = PTsn[:TS, hh, :]
                                        mm(PS[bo][:, col], lh, rhs, m == 0, m == 3, [rVCs, rVLH, rPTsc, rPTsn, rCB], [rPS[bo]])
                                        m += 1
                            rd, rrd = RD[jl]
                            ts("dve", rd[:, 0:16], PS[bo][:, 16:32], ESK[:, l, jj:jj + 1], None, ALU.add, None, [rPS[bo], rESK], [rrd])
                            recip(rd[:, 0:16], rd[:, 0:16], [rrd], [rrd])
                            tt("dve", BR[:, jj, qc], PS[bo][:, 0:16], rd[:, 0:16], ALU.mult, [rPS[bo], rrd], [rBR])

            def ssm(b, l):
                areset()
                DT, rDT = aalloc([9, 16], F32)
                DTA, rDTA = aalloc([9, 16], F32)
                tmpd, rtmpd = aalloc([16], F32)
                wdt, rwdt = wnext(("in", l, "dt"))
                dtb = cf("dtb", l)
                for i in range(9):
                    rows = 128 if i < 8 else TS
                    t0 = i * 128
                    bank = nextbank([0, 1, 2, 3])
                    for k in range(8):
                        mm(PS[bank][:rows, 0:16], H[:, k, t0:t0 + rows], wdt[:, k, 0:16], k == 0, k == 7, [rwdt] + rHof(t0, rows), [rPS[bank]])
                    tt("dve", tmpd[:rows, :], PS[bank][:rows, 0:16], dtb[:rows, :], ALU.add, [rPS[bank], rCF], [rtmpd])
                    act(tmpd[:rows, :], tmpd[:rows, :], AF.Exp, [rtmpd], [rtmpd])
                    act(DT[:rows, i, :], tmpd[:rows, :], AF.Ln, [rtmpd], [rDT], bias=1.0)
                    tt("dve", DTA[:rows, i, :], DT[:rows, i, :], AROW[:rows, l, :], ALU.mult, [rDT, rAROW], [rDTA])
                DTAH, rDTAHL = aalloc([9, 16], BF16)
                DTAL, _ = aalloc([9, 16], BF16)
                dres, rdres = aalloc([9, 16], F32)
                DTAHf, _ = aalloc([9, 16], F32)
                memset("dve", DTA[:, :, :], 0.0, [rDTA]) if False else None
                for (r0, r1, tiles) in ((0, 128, slice(0, 8)), (0, TS, slice(8, 9))):
                    cpy("dve", DTAH[r0:r1, tiles, :], DTA[r0:r1, tiles, :], [rDTA], [rDTAHL])
                    tt("dve", dres[r0:r1, tiles, :], DTA[r0:r1, tiles, :], DTAH[r0:r1, tiles, :], ALU.subtract, [rDTA, rDTAHL], [rdres])
                    cpy("dve", DTAL[r0:r1, tiles, :], dres[r0:r1, tiles, :], [rdres], [rDTAHL])
                    cpy("dve", DTAHf[r0:r1, tiles, :], DTAH[r0:r1, tiles, :], [rDTAHL], [rdres])
                mark = arena_top[0]
                for g in range(2):
                    if g == 1:
                        P.barrier()
                        arena_top[0] = mark
                    ssm_group(b, l, g, DT, rDT, DTAH, DTAL, rDTAHL, DTAHf, dres, rdres)
                P.barrier()
                arena_top[0] = mark
                rstd, rrstd = aalloc([T], F32)
                sqs = [aalloc([T], BF16) for _ in range(2)]
                banks = [0, 1, 2]
                for c in range(8):
                    sq, rsq = sqs[c % 2]
                    act(sq[:, :], BR[:, c, :], AF.Square, [rBR], [rsq])
                    for ti, (t0, tn) in enumerate(NT):
                        mm(PS[banks[ti]][:, :tn], cb("ones"), sq[:, t0:t0 + tn], c == 0, c == 7, [rsq, rCB], [rPS[banks[ti]]])
                for ti, (t0, tn) in enumerate(NT):
                    ts("dve", rstd[:, t0:t0 + tn], PS[banks[ti]][:, :tn], 1.0 / D, EPS, ALU.mult, ALU.add, [rPS[banks[ti]]], [rrstd])
                rsqrt_inplace(rstd[:, :], rrstd)
                nw = cf("normw", l)
                for c in range(8):
                    stt("dve", BR[:, c, :], BR[:, c, :], nw[:, c:c + 1], rstd[:, :], ALU.mult, ALU.mult, [rBR, rrstd, rCF], [rBR])

            def ssm_group(b, l, g, DT, rDT, DTAH, DTAL, rDTAHL, DTAHf, DTALf, rDTf):
                SZ, rSZ = aalloc([4, T], BF16)
                XC, rXC = aalloc([6, T], BF16)
                mark_c = arena_top[0]
                dsk_l = cf("dskip", l)
                PREs = [aalloc([3 + TP], F32) for _ in range(2)]
                PRSs = [aalloc([2, 19], F32) for _ in range(2)]
                ACCs = [aalloc([T], F32) for _ in range(2)]
                convw = cf("convw", l)
                convb = cf("convb", l)
                for j in range(4):
                    wz, rwz = wnext(("in", l, "z", 4 * g + j))

                    def ez(bank, ti, t0, tn, j=j):
                        act(SZ[:, j, t0:t0 + tn], PS[bank][:, :tn], AF.Silu, [rPS[bank]], [rSZ])
                    proj_fm(wz, rwz, 8, 0, H, rHof, [0, 1, 2, 3, 4, 5, 6, 7], ez)
                flist = [("x", 4 * g + j, 4 * g + j) for j in range(4)] + [("B", g, 8 + g), ("C", g, 10 + g)]
                for fi, (kind, idx, f) in enumerate(flist):
                    wx, rwx = wnext(("in", l, kind, idx))
                    PRE, rPRE = PREs[fi % 2]
                    PRS, rPRS = PRSs[fi % 2]
                    ACC, rACC = ACCs[fi % 2]
                    if b == 0:
                        memset("dve", PRE[:, 0:3], 0.0, [rPRE])
                    else:
                        cpy("dve", PRE[:, 0:3], CCAR[:, f, :], [rCCAR], [rPRE])
                    for s in range(2):
                        P.dma("sp", ldsl, PRS[:, s, 0:3], cs_d[2 * b + s, l, :, f, :], writes=[rPRS])

                    def ex(bank, ti, t0, tn, PRE=PRE, rPRE=rPRE, PRS=PRS, rPRS=rPRS):
                        pe_ = min(t0 + tn, TP)
                        if pe_ > t0:
                            cpy("act", PRE[:, 3 + t0:3 + pe_], PS[bank][:, 0:pe_ - t0], [rPS[bank]], [rPRE])
                        if t0 + tn > TP:
                            assert t0 <= TP and t0 + tn == T
                            cpy("act", PRS[:, :, 3:19], PS[bank][:, TP - t0:TP - t0 + TS].rearrange("p (s t) -> p s t", s=2), [rPS[bank]], [rPRS])
                    proj_fm(wx, rwx, 8, 0, H, rHof, [0, 1, 2, 3, 4, 5, 6, 7], ex)
                    if b == 0:
                        cpy("dve", CCAR[:, f, :], PRE[:, TP:TP + 3], [rPRE], [rCCAR])
                    else:
                        odma(cp_o[l, :, f, :], PRE[:, TP:TP + 3], [rPRE])
                    for s in range(2):
                        odma(cso_o[l, 2 * b + s, :, f, :], PRS[:, s, 16:19], [rPRS])
                    wc = convw[:, f * 4:(f + 1) * 4]
                    ts("dve", ACC[:, 0:TP], PRE[:, 3:3 + TP], wc[:, 3:4], convb[:, f:f + 1], ALU.mult, ALU.add, [rPRE, rCF], [rACC])
                    accs = ACC[:, TP:T].rearrange("p (s t) -> p s t", s=2)
                    ts("dve", accs, PRS[:, :, 3:19], wc[:, 3:4], convb[:, f:f + 1], ALU.mult, ALU.add, [rPRS, rCF], [rACC])
                    for i in range(3):
                        stt("dve", ACC[:, 0:TP], PRE[:, i:i + TP], wc[:, i:i + 1], ACC[:, 0:TP], ALU.mult, ALU.add, [rPRE, rACC, rCF], [rACC])
                        stt("dve", accs, PRS[:, :, i:i + 16], wc[:, i:i + 1], accs, ALU.mult, ALU.add, [rPRS, rACC, rCF], [rACC])
                    act(XC[:, fi, :], ACC[:, :], AF.Silu, [rACC], [rXC])

                P.barrier()
                arena_top[0] = mark_c
                NH = 8
                RENG = "dve"
                RH, rRH = aalloc([NH, 128], BF16)
                RL, rRL = aalloc([NH, 128], BF16)
                DEC, rDEC = aalloc([NH, 128], BF16)
                EROW, rEROW = aalloc([NH, 128], F32)
                CBM, rCBM = aalloc([128], F32)
                MTs = [aalloc([NH, 128], BF16) for _ in range(2)]
                CEXPs = [aalloc([NH, 128], BF16) for _ in range(2)]
                XBTs = [aalloc([5, 128], BF16) for _ in range(2)]
                XDTs = [aalloc([4, 2, 128], BF16) for _ in range(2)]
                XDWs = [[aalloc([512], BF16)] for _ in range(2)]
                ELAs = [aalloc([2, NH], F32) for _ in range(2)]
                DTW, rDTW = aalloc([NH], F32)
                BMs = [aalloc([2, 128], BF16) for _ in range(2)]
                DG, rDG = aalloc([4, 128], BF16)
                for j in range(4):
                    ts("dve", DG[:, j, :], cb("ident"), dsk_l[:, 4 * g + j:4 * g + j + 1], None, ALU.mult, None, [rCB, rCF], [rDG])
                WW, rWW = aalloc([NH], F32)
                HT, rHT = aalloc([512], F32)
                HLOS = [aalloc([4, 2, 128], BF16) for _ in range(2)]
                HS0 = [aalloc([512], F32) for _ in range(2)]
                for _h, _r in XDTs + HLOS:
                    memset("dve", _h[:, :, :, :], 0.0, [_r])
                hsl = slice(g * 8, (g + 1) * 8)
                dsk = cf("dskip", l)
                if b == 0:
                    memset("dve", HST[:, g, :], 0.0, [rHST[g]])

                def padview(buf, rows):
                    a = buf[0:rows, 0, 0, 0:64]
                    pst = a.ap[0][0]
                    return bass.AP(a.tensor, a.offset, [[pst, rows], [256, 4], [192, 2], [1, 64]])

                def pair_hdr(pi):
                    samp = pi == 8
                    rows = TS if samp else 128
                    rs = slice(0, rows)
                    t0 = pi * 128
                    tc = slice(t0, t0 + rows)
                    par_ = pi % 2
                    MT, rMT = MTs[par_]
                    CEXP, rCEXP = CEXPs[par_]
                    XBT, rXBT = XBTs[par_]
                    XDT, rXDT = XDTs[par_]
                    ELA, rELA = ELAs[par_]
                    m01 = cf("m01s")[rs, 0:rows] if samp else cf("m01")[:, :]
                    m01b = cb("m01s")[rs, 0:rows] if samp else cb("m01")[:, :]
                    sub = 16 if samp else 64
                    mcol = cf("mk") if samp else cf("mkp")
                    return locals()

                def front1(pi, part):
                    L = pair_hdr(pi)
                    samp, rows, rs, t0, tc, par_ = L['samp'], L['rows'], L['rs'], L['t0'], L['tc'], L['par_']
                    MT, rMT, CEXP, rCEXP, XBT, rXBT, XDT, rXDT, ELA, rELA = (L[k] for k in ('MT', 'rMT', 'CEXP', 'rCEXP', 'XBT', 'rXBT', 'XDT', 'rXDT', 'ELA', 'rELA'))
                    m01, m01b, sub, mcol = L['m01'], L['m01b'], L['sub'], L['mcol']
                    if part == 'b':
                        for hb in range(2):
                            hcols = slice(hb * 4, (hb + 1) * 4)
                            od = PS[hb][rs, 0:4 * rows].rearrange("p (h t) -> p h t", h=4)
                            oc = PS[2 + hb][:, 0:4 * rows].rearrange("p (h t) -> p h t", h=4)
                            act(DEC[rs, hcols, 0:rows], od, AF.Exp, [rPS[hb]], [rDEC])
                            act(EROW[:, hcols, 0:rows], oc, AF.Exp, [rPS[2 + hb]], [rEROW])
                        bcb = 4
                        mm(PS[bcb][rs, 384:384 + rows], XC[:, 4, tc], XC[:, 5, tc], True, True, [rXC], [rPS[bcb]])
                        return
                    btp = 4
                    for ci in range(5):
                        tp(PS[btp][rs, :].bitcast(BF16)[:, ci * 128:(ci + 1) * 128], XC[:, ci, tc], cb("ident"),
                           [rXC, rCB], [rPS[btp]])
                    cpy("act", XBT[rs, :, :], PS[btp][rs, :].bitcast(BF16)[:, 0:640].rearrange("p (c m) -> p c m", c=5), [rPS[btp]], [rXBT])
                    for (Rb, rRb, Dsrc) in ((RH, rRH, DTAHf), (RL, rRL, DTALf)):
                        for h in range(NH):
                            act(Rb[rs, h, 0:rows], m01b, AF.Copy, [rDTf, rCB], [rRb], scale=Dsrc[rs, pi, g * 8 + h:g * 8 + h + 1])
                    for hb in range(2):
                        hcols = slice(hb * 4, (hb + 1) * 4)
                        od = PS[hb][rs, 0:4 * rows].rearrange("p (h t) -> p h t", h=4)
                        oc = PS[2 + hb][:, 0:4 * rows].rearrange("p (h t) -> p h t", h=4)
                        mm(od, cb("su")[rs, 0:rows], RH[rs, hcols, 0:rows], True, False, [rRH, rCB], [rPS[hb]])
                        mm(od, cb("su")[rs, 0:rows], RL[rs, hcols, 0:rows], False, True, [rRL, rCB], [rPS[hb]])
                        mm(oc, cb("ones")[rs, :], RH[rs, hcols, 0:rows], True, False, [rRH, rCB], [rPS[2 + hb]])
                        mm(oc, cb("ones")[rs, :], RL[rs, hcols, 0:rows], False, True, [rRL, rCB], [rPS[2 + hb]])

                def front2(pi, part):
                    L = pair_hdr(pi)
                    samp, rows, rs, t0, tc, par_ = L['samp'], L['rows'], L['rs'], L['t0'], L['tc'], L['par_']
                    MT, rMT, CEXP, rCEXP, XBT, rXBT, XDT, rXDT, ELA, rELA = (L[k] for k in ('MT', 'rMT', 'CEXP', 'rCEXP', 'XBT', 'rXBT', 'XDT', 'rXDT', 'ELA', 'rELA'))
                    m01, m01b, sub, mcol = L['m01'], L['m01b'], L['sub'], L['mcol']
                    bcb = 4
                    if part == 'a':
                        tt("dve", CBM[rs, 0:rows], PS[bcb][rs, 384:384 + rows], m01, ALU.mult, [rPS[bcb], rCF], [rCBM])
                        tt("dve", MT[rs, :, 0:rows], DEC[rs, :, 0:rows], CBM[rs, 0:rows].unsqueeze(1).to_broadcast([rows, NH, rows]),
                           ALU.mult, [rDEC, rCBM], [rMT])
                        tt(RENG, CEXP[:, :, 0:rows], EROW[:, :, 0:rows], XC[:, 5, tc].unsqueeze(1).to_broadcast([128, NH, rows]),
                           ALU.mult, [rEROW, rXC], [rCEXP])
                        return
                    for sc in range(2):
                        cpy("act", ELA[:, sc, :], EROW[:, :, (sc + 1) * sub - 1], [rEROW], [rELA])
                    if samp:
                        ts("dve", WW[rs, :], DEC[rs, :, 15], mcol[rs, 0:1], None, ALU.mult, None, [rDEC, rCF], [rWW])
                        stt("dve", WW[rs, :], DEC[rs, :, 31], mcol[rs, 1:2], WW[rs, :], ALU.mult, ALU.add, [rDEC, rCF, rWW], [rWW])
                    else:
                        cpy("act", WW[0:64, :], DEC[0:64, :, 63], [rDEC], [rWW])
                        cpy("act", WW[64:128, :], DEC[64:128, :, 127], [rDEC], [rWW])
                    tt("dve", DTW[rs, :], WW[rs, :], DT[rs, pi, hsl], ALU.mult, [rWW, rDT], [rDTW])
                    BM, rBM = BMs[par_]
                    for sc in range(2):
                        ts("dve", BM[rs, sc, :], XBT[rs, 4, :], mcol[rs, sc:sc + 1], None, ALU.mult, None, [rXBT, rCF], [rBM])
                    xtok4 = XBT[rs, 0:4, :].rearrange("p c (v d) -> p c v d", v=2)
                    dt4 = DT[rs, pi, hsl].rearrange("p (c v) -> p c v", v=2)
                    tt("dve", padview(XDT, rows), xtok4, dt4.unsqueeze(3).to_broadcast([rows, 4, 2, 64]), ALU.mult, [rXBT, rDT], [rXDT])
                    xtok8 = XBT[rs, 0:4, :].rearrange("p c (v d) -> p (c v) d", v=2)
                    xw, rxw = XDWs[par_][0]
                    tt("dve", xw[rs, :].rearrange("p (h d) -> p h d", h=NH), xtok8,
                       DTW[rs, :].unsqueeze(2).to_broadcast([rows, NH, 64]), ALU.mult, [rXBT, rDTW], [rxw])
                    BM, rBM = BMs[par_]
                    for sc in range(2):
                        sb_ = (7, 5)[sc]
                        mm(PS[sb_][:, :], BM[rs, sc, :], xw[rs, :], True, True, [rBM, rxw], [rPS[sb_]])

                def back(pi, part):
                    L = pair_hdr(pi)
                    samp, rows, rs, t0, tc, par_ = L['samp'], L['rows'], L['rs'], L['t0'], L['tc'], L['par_']
                    MT, rMT, CEXP, rCEXP, XBT, rXBT, XDT, rXDT, ELA, rELA = (L[k] for k in ('MT', 'rMT', 'CEXP', 'rCEXP', 'XBT', 'rXBT', 'XDT', 'rXDT', 'ELA', 'rELA'))
                    m01, m01b, sub, mcol = L['m01'], L['m01b'], L['sub'], L['mcol']
                    BM, rBM = BMs[par_]
                    xw, rxw = XDWs[par_][0]
                    by = 6
                    bs_ = 7
                    for sc in ({'s0': (0,), 's1': (1,)}.get(part, ())):
                        hlo, rhlo = HLOS[sc]
                        if samp:
                            hsrc, rhsrc = HS0[sc]
                            P.dma("sp", ldsl, hsrc[:, :], hs_d[2 * b + sc, l, :, g * 512:(g + 1) * 512], writes=[rhsrc])
                            hcur, rhcur = hsrc, rhsrc
                        else:
                            hcur, rhcur = HST[:, g, :], rHST[g]
                        cpy("act", padview(hlo, 128), hcur.rearrange("p (j v d) -> p j v d", j=4, v=2), [rhcur], [rhlo])
                        bs_ = (7, 5)[sc]
                        tt("dve", HT[:, :].rearrange("p (h d) -> p h d", h=NH), hcur.rearrange("p (h d) -> p h d", h=NH),
                           ELA[:, sc, :].unsqueeze(2).to_broadcast([128, NH, 64]), ALU.mult, [rhcur, rELA], [rHT])
                        if samp:
                            tt("dve", hcur, HT[:, :], PS[bs_][:, :], ALU.add, [rHT, rPS[bs_]], [rhcur])
                            odma(hso_o[l, 2 * b + sc, :, g * 512:(g + 1) * 512], hcur, [rhcur])
                        else:
                            tt("dve", HST[:, g, :], HT[:, :], PS[bs_][:, :], ALU.add, [rHT, rPS[bs_]], [rHST[g]])
                    for j in (range(4) if part == 'y' else ()):
                        for par in range(2):
                            mm(PS[by][:, j * rows:(j + 1) * rows], XDT[rs, j, par, :], MT[rs, 2 * j + par, 0:rows], par == 0, False,
                               [rXDT, rMT], [rPS[by]])
                        mm(PS[by][:, j * rows:(j + 1) * rows], DG[:, j, :], XC[:, j, tc], False, False, [rDG, rXC], [rPS[by]])
                        for sc in range(2):
                            hlo, rhlo = HLOS[sc]
                            for par in range(2):
                                mm(PS[by][:, j * rows + sc * sub:j * rows + (sc + 1) * sub], hlo[:, j, par, :],
                                   CEXP[:, 2 * j + par, sc * sub:(sc + 1) * sub], False, par == 1 and sc == 1, [rhlo, rCEXP], [rPS[by]])
                    if part == 'evac':
                        tt("dve", BR[:, 4 * g:4 * g + 4, tc], PS[by][:, 0:4 * rows].rearrange("p (j t) -> p j t", j=4), SZ[:, :, tc], ALU.mult,
                           [rPS[by], rSZ], [rBR])

                front1(0, 'a')
                front1(0, 'b')
                front2(0, 'a')
                front2(0, 'b')
                for pi in range(9):
                    nxt = pi + 1 < 9
                    back(pi, 's0')
                    if nxt:
                        front1(pi + 1, 'a')
                    back(pi, 's1')
                    if nxt:
                        front1(pi + 1, 'b')
                    back(pi, 'y')
                    if nxt:
                        front2(pi + 1, 'a')
                    back(pi, 'evac')
                    if nxt:
                        front2(pi + 1, 'b')
                if b == 1:
                    odma(hp_o[l, :, g * 512:(g + 1) * 512], HST[:, g, :], [rHST[g]])

            def gelu_A(src, rsrc, tmpa, rta):
                act(tmpa, src, AF.Square, [rsrc], [rta])
                ts("dve", tmpa, tmpa, 0.044715, 1.0, ALU.mult, ALU.add, [rta], [rta])
                tt("dve", tmpa, tmpa, src, ALU.mult, [rta, rsrc], [rta])

            def gelu_B(dst, src, rsrc, rdst, tmpa, rta, tmpb, rtb):
                act(tmpb, tmpa, AF.Sigmoid, [rta], [rtb], scale=2.0 * math.sqrt(2.0 / math.pi))
                tt("dve", dst, tmpb, src, ALU.mult, [rtb, rsrc], [rdst])

            def gmlp(b, l):
                areset()
                GU, rGU = aalloc([4, T], BF16)
                VT, rVT = aalloc([9, 512], BF16)
                GT, rGT = aalloc([GT_W], F32)
                GW, rGW = aalloc([GW_W], BF16)
                raws = [aalloc([512], F32) for _ in range(3)]
                tas = [aalloc([512], F32) for _ in range(2)]
                tbs = [aalloc([512], F32) for _ in range(2)]
                gcnt = [0]
                pend = []
                gls = [aalloc([512], F32) for _ in range(9)]
                vfs = [aalloc([512], F32) for _ in range(2)]
                sts = [aalloc([8], F32) for _ in range(2)]
                ags = [aalloc([4], F32) for _ in range(2)]
                MXs = [aalloc([512], F32) for _ in range(2)]
                P.dma("sp", ldsl, GT[:, :], gt_d[l], writes=[rGT])
                pool_arena_dma(GW[:, :], gw_d[l], [rGW])
                for j in range(4):
                    wu, rwu = wnext(("in", l, "gu", j))

                    def eu(bank, ti, t0, tn, j=j):
                        raw, rraw = raws[gcnt[0] % 3]
                        ta, rta = tas[gcnt[0] % 2]
                        tb_, rtb = tbs[gcnt[0] % 2]
                        gcnt[0] += 1
                        cpy("act", raw[:, :tn], PS[bank][:, :tn], [rPS[bank]], [rraw])
                        gelu_A(raw[:, :tn], rraw, ta[:, :tn], rta)
                        if pend:
                            pend.pop()()
                        pend.append(lambda: gelu_B(GU[:, j, t0:t0 + tn], raw[:, :tn], rraw, rGU, ta[:, :tn], rta, tb_[:, :tn], rtb))
                    proj_fm(wu, rwu, 8, 0, H, rHof, [0, 1, 2, 3], eu)
                if pend:
                    pend.pop()()
                wvs = [wnext(("in", l, "gv", j)) for j in range(4)]
                lng = GT[:, 0:512]
                lnb = GT[:, 512:1024]
                def gvA(i):
                    rows = 128 if i < 8 else TS
                    rs = slice(0, rows)
                    t0 = i * 128
                    bank = nextbank([0, 1, 2, 3])
                    for j in range(4):
                        wv_, rwv_ = wvs[j]
                        for k in range(8):
                            mm(PS[bank][rs, j * 128:(j + 1) * 128], H[:, k, t0:t0 + rows], wv_[:, k, :], k == 0, k == 7, [rwv_] + rHof(t0, rows), [rPS[bank]])
                    raw, rraw = raws[i % 3]
                    ta, rta = tas[i % 2]
                    cpy("act", raw[rs, :], PS[bank][rs, :], [rPS[bank]], [rraw])
                    gelu_A(raw[rs, :], rraw, ta[rs, :], rta)

                def gvB(i):
                    rows = 128 if i < 8 else TS
                    rs = slice(0, rows)
                    raw, rraw = raws[i % 3]
                    ta, rta = tas[i % 2]
                    tb_, rtb = tbs[i % 2]
                    gl, rgl = gls[i]
                    gelu_B(gl[rs, :], raw[rs, :], rraw, rgl, ta[rs, :], rta, tb_[rs, :], rtb)

                for i in range(10):
                    if i < 9:
                        gvA(i)
                    if i >= 1:
                        gvB(i - 1)

                def lnC(i):
                    rows = 128 if i < 8 else TS
                    rs = slice(0, rows)
                    gl, rgl = gls[i]
                    st, rst = sts[i % 2]
                    ag, rag = ags[i % 2]
                    P.add("dve", lambda e, rs=rs, st=st, gl=gl: e.bn_stats(out=st[rs, 0:6], in_=gl[rs, :]), reads=[rgl], writes=[rst])
                    P.add("dve", lambda e, rs=rs, st=st, ag=ag: e.bn_aggr(out=ag[rs, 0:2], in_=st[rs, 0:6]), reads=[rst], writes=[rag])
                    ts("dve", ag[rs, 2:3], ag[rs, 1:2], EPS, None, ALU.add, None, [rag], [rag])

                def lnD(i):
                    rows = 128 if i < 8 else TS
                    rs = slice(0, rows)
                    ag, rag = ags[i % 2]
                    rsqrt_inplace(ag[rs, 2:3], rag)

                def lnE(i):
                    rows = 128 if i < 8 else TS
                    rs = slice(0, rows)
                    gl, rgl = gls[i]
                    vf, rvf = vfs[i % 2]
                    ag, rag = ags[i % 2]
                    ts("dve", vf[rs, :], gl[rs, :], ag[rs, 0:1], ag[rs, 2:3], ALU.subtract, ALU.mult, [rgl, rag], [rvf])
                    tt("dve", vf[rs, :], vf[rs, :], lng[rs, :], ALU.mult, [rvf, rGT], [rvf])
                    tt("dve", vf[rs, :], vf[rs, :], lnb[rs, :], ALU.add, [rvf, rGT], [rvf])
                    cpy("act", VT[rs, i, :], vf[rs, :], [rvf], [rVT])
                    if i == 8:
                        odma(gv_o[l, b], vf[rs, :], [rvf])

                lnC(0)
                for i in range(9):
                    if i + 1 < 9:
                        lnC(i + 1)
                    lnD(i)
                    lnE(i)
                tt("dve", GW[:, 0:512].rearrange("p (g t) -> p g t", g=4), GW[:, 0:512].rearrange("p (g t) -> p g t", g=4),
                   cf("m01g")[:, :].unsqueeze(1).to_broadcast([128, 4, 128]), ALU.mult, [rGW, rCF], [rGW])
                tt("dve", GW[0:32, 512:640].rearrange("p (g t) -> p g t", g=4), GW[0:32, 512:640].rearrange("p (g t) -> p g t", g=4),
                   cf("m01s")[0:32, :].unsqueeze(1).to_broadcast([32, 4, 32]), ALU.mult, [rGW, rCF], [rGW])
                for i in range(9):
                    rows = 128 if i < 8 else TS
                    rs = slice(0, rows)
                    t0 = i * 128
                    tc = slice(t0, t0 + rows)
                    bank = nextbank([4, 5, 6, 7])
                    for gg in range(4):
                        if i < 8:
                            rhs = GW[:, gg * 128:(gg + 1) * 128]
                        else:
                            rhs = GW[0:32, 512 + gg * 32:512 + (gg + 1) * 32]
                        mm(PS[bank][:, gg * rows:(gg + 1) * rows], VT[rs, i, gg * 128:(gg + 1) * 128], rhs, True, True, [rVT, rGW], [rPS[bank]])
                    if i < 8:
                        bias = GT[:, 1024:1536]
                    else:
                        bias = GT[:, 1536:1664]
                    MX, rMX = MXs[i % 2]
                    tt("dve", MX[:, 0:4 * rows], PS[bank][:, 0:4 * rows], bias, ALU.add, [rPS[bank], rGT], [rMX])
                    tt("dve", BR[:, 0:4, tc], MX[:, 0:4 * rows].rearrange("p (g t) -> p g t", g=4), GU[:, :, tc], ALU.mult, [rMX, rGU], [rBR])

            def out_proj(b, l):
                areset()
                for c in range(8):
                    wo, rwo = wnext(("wo", l, c))

                    def eo(bank, ti, t0, tn, c=c):
                        for (g0, gn, s) in groups(b):
                            lo = max(g0, t0)
                            hi = min(g0 + gn, t0 + tn)
                            if lo >= hi:
                                continue
                            stt("dve", X[b][:, c, lo:hi], PS[bank][:, lo - t0:hi - t0], modcol(l, 2, c, s), X[b][:, c, lo:hi],
                                ALU.mult, ALU.add, [rPS[bank], rMOD, rX[b][c]], [rX[b][c]])
                    proj_fm(wo, rwo, 8, 0, MG, rMG, [0, 1, 2, 3, 4, 5, 6, 7], eo)

            def ffn(b, l):
                areset()
                norm_mod(b, l, 1, 3)
                mark = arena_top[0]
                for hh in range(2):
                    P.barrier()
                    arena_top[0] = mark
                    Fh, rF = aalloc([16, T], BF16)
                    rls = [aalloc([512], F32) for _ in range(2)]
                    cnt = [0]
                    for c in range(16):
                        w1, rw1 = wnext(("f1", l, hh * 16 + c))

                        def e1(bank, ti, t0, tn, c=c):
                            rl, rrl = rls[cnt[0] % 2]
                            cnt[0] += 1
                            act(rl[:, :tn], PS[bank][:, :tn], AF.Relu, [rPS[bank]], [rrl])
                            tt("dve", Fh[:, c, t0:t0 + tn], rl[:, :tn], rl[:, :tn], ALU.mult, [rrl], [rF])
                        proj_fm(w1, rw1, 8, 0, H, rHof, [0, 1, 2, 3, 4, 5, 6, 7], e1)
                    for c in range(8):
                        w2a, rw2a = wnext(("f2", l, hh, c, 0))
                        w2b, rw2b = wnext(("f2", l, hh, c, 1))
                        for ti, (t0, tn) in enumerate(NT):
                            bank = nextbank([0, 1, 2, 3, 4, 5, 6, 7])
                            for k in range(16):
                                w2, rw2 = (w2a, rw2a) if k < 8 else (w2b, rw2b)
                                mm(PS[bank][:, :tn], w2[:, k % 8, :], Fh[:, k, t0:t0 + tn], k == 0, k == 15, [rw2, rF], [rPS[bank]])
                            for (g0, gn, s) in groups(b):
                                lo = max(g0, t0)
                                hi = min(g0 + gn, t0 + tn)
                                if lo >= hi:
                                    continue
                                stt("dve", X[b][:, c, lo:hi], PS[bank][:, lo - t0:hi - t0], modcol(l, 5, c, s), X[b][:, c, lo:hi],
                                    ALU.mult, ALU.add, [rPS[bank], rMOD, rX[b][c]], [rX[b][c]])

            def final(b):
                areset()
                rstd, rrstd = aalloc([T], F32)
                rms_rstd(b, rstd, rrstd)
                gf = cf("gfin")
                for c in range(8):
                    stt("dve", X[b][:, c, :], X[b][:, c, :], gf[:, c:c + 1], rstd[:, :], ALU.mult, ALU.mult, [rX[b][c], rrstd, rCF], [rX[b][c]])
                    odma(yT_o[b, :, c, :], X[b][:, c, :], [rX[b][c]])

            phase = [0]

            def go():
                phase[0] += 1
                return phase[0] <= self.limit

            for l in range(2):
                for b in range(2):
                    if go():
                        areset()
                        norm_mod(b, l, 0, 0)
                    if go():
                        attention(b, l)
                    if go():
                        merge_branch(l, 0, 4, True)
                    if go():
                        ssm(b, l)
                    if go():
                        merge_branch(l, 1, 8, False)
                    if go():
                        gmlp(b, l)
                    if go():
                        merge_branch(l, 2, 4, True)
                    if go():
                        out_proj(b, l)
                    if go():
                        ffn(b, l)
                    if l == 1 and go():
                        final(b)
            if self.limit < 1000:
                for b in range(2):
                    for c in range(8):
                        odma(yT_o[b, :, c, :], X[b][:, c, :], [rX[b][c]])
            lastout = [o for o in P.ops if o.slot is osl][-1]
            P.add("sp", None, deps=[lastout])
            P.emit(sems, block)
        return nc


def _chunk_from_cols(W, cols, krows=None):
    K = W.shape[0]
    if krows is None:
        krows = np.arange(8)
    out = np.zeros((128, 8, 128), np.float32)
    cols = np.asarray(cols)
    valid = cols >= 0
    for ki, k in enumerate(krows):
        blk = W[k * 128:(k + 1) * 128, :]
        out[:, ki, valid] = blk[:, cols[valid]]
    return out


def _make_chunk(key, w):
    name = key[0]
    l = key[1]
    m = np.arange(128)
    if name == "ada":
        return _chunk_from_cols(w["w_ada"][l], key[2] * 128 + m)
    if name == "in":
        W = w["w_in"][l]
        kind = key[2]
        if kind == "q":
            cols = O_Q + key[3] * 128 + m
        elif kind == "qr":
            cols = np.array([(O_Q + key[3] * 128 + (mm_ // 64) * 64 + _rot_src(mm_ % 64)) if _rot_src(mm_ % 64) >= 0 else -1 for mm_ in m])
        elif kind == "k":
            cols = O_K + key[3] * 64 + (m % 64)
        elif kind == "kr":
            cols = np.array([(O_K + key[3] * 64 + _rot_src(mm_ % 64)) if _rot_src(mm_ % 64) >= 0 else -1 for mm_ in m])
        elif kind == "v":
            cols = O_V + m
        elif kind == "z":
            cols = O_Z + key[3] * 128 + m
        elif kind == "x":
            cols = O_X + key[3] * 128 + m
        elif kind == "B":
            cols = O_B + key[3] * 128 + m
        elif kind == "C":
            cols = O_C + key[3] * 128 + m
        elif kind == "dt":
            cols = np.where(m < 16, O_DT + m, -1)
        elif kind == "gu":
            cols = O_GU + key[3] * 128 + m
        elif kind == "gv":
            cols = O_GV + key[3] * 128 + m
        elif kind == "gate":
            cols = O_G + key[3] * 1024 + key[4] * 128 + m
        else:
            raise KeyError(key)
        return _chunk_from_cols(W, cols)
    if name in ("ao", "go"):
        W = w["w_attn_o" if name == "ao" else "w_gm_o"][l]
        out = np.zeros((128, 8, 128), np.float32)
        for c2 in range(2):
            c = 2 * key[2] + c2
            for k in range(4):
                out[:, c2 * 4 + k, :] = W[k * 128:(k + 1) * 128, c * 128:(c + 1) * 128]
        return out
    if name == "so":
        return _chunk_from_cols(w["w_ssm_o"][l], key[2] * 128 + m)
    if name == "wo":
        return _chunk_from_cols(w["w_out"][l], key[2] * 128 + m)
    if name == "f1":
        return _chunk_from_cols(w["w_ff1"][l], key[2] * 128 + m)
    if name == "f2":
        hh, c, h2 = key[2], key[3], key[4]
        return _chunk_from_cols(w["w_ff2"][l], c * 128 + m, krows=hh * 16 + h2 * 8 + np.arange(8))
    raise KeyError(key)


_CACHE = {}


def _get_program():
    if "nc" not in _CACHE:
        import os
        lim = int(os.environ.get("KLIMIT", "100000"))
        bld = Builder(lim)
        n = NWCH
        nc = bld.build(n)
        assert len(bld.keys) <= n, len(bld.keys)
        while len(bld.keys) < n:
            bld.keys.append(bld.keys[0])
        _CACHE["nc"] = nc
        _CACHE["keys"] = bld.keys
    return _CACHE["nc"], _CACHE["keys"]


def _prep(inp, keys, cores=range(8)):
    f32 = np.float32
    wst = np.stack([_make_chunk(k, inp) for k in keys]).astype(f32)
    cfa = np.zeros((128, NCF), f32)
    p = np.arange(128)
    for l in range(2):
        def put(name, arr, l=l):
            o, wd = LC[name]
            cfa[:, l * LCW + o:l * LCW + o + wd] = arr
        put("gmix", inp["g_mix"][l].reshape(8, 128).T)
        put("gff", inp["g_ff"][l].reshape(8, 128).T)
        cw = inp["conv_w"][l]
        put("convw", cw.reshape(4, 12, 128).transpose(2, 1, 0).reshape(128, 48))
        put("convb", inp["conv_b"][l].reshape(12, 128).T)
        put("normw", inp["ssm_norm_w"][l].reshape(8, 128).T)
        put("dskip", np.stack([inp["d_skip"][l][2 * c + p // 64] for c in range(8)], axis=1))
        put("sink", np.stack([inp["sinks"][l][2 * j + p // 64] for j in range(4)], axis=1))
        put("bada", inp["b_ada"][l].reshape(48, 128).T)
        put("dtb", np.broadcast_to(inp["dt_bias"][l][None, :], (128, 16)))
        put("alog", np.broadcast_to(inp["a_log"][l][None, :], (128, 16)))

    def putg(name, arr):
        o, wd = GC[name]
        cfa[:, o:o + wd] = arr
    putg("gfin", inp["g_final"].reshape(8, 128).T)
    u = p[:, None]
    t = p[None, :]
    putg("m01", ((u <= t) & (u // 64 == t // 64)).astype(f32))
    putg("su", (u > t).astype(f32))
    putg("onesf", np.ones((128, 128), f32))
    m01s = np.zeros((128, 32), f32)
    uu = np.arange(32)[:, None]
    tt_ = np.arange(32)[None, :]
    m01s[:32] = ((uu <= tt_) & (uu // 16 == tt_ // 16)).astype(f32)
    putg("m01s", m01s)
    mk = np.zeros((128, 2), f32)
    mk[0:16, 0] = 1.0
    mk[16:32, 1] = 1.0
    putg("mk", mk)
    putg("m01g", (u <= t).astype(f32))
    mkp = np.zeros((128, 2), f32)
    mkp[0:64, 0] = 1.0
    mkp[64:128, 1] = 1.0
    putg("mkp", mkp)
    cba = np.zeros((128, NCB), f32)
    cba[:, BC["ident"][0]:BC["ident"][0] + 128] = np.eye(128, dtype=f32)
    cba[:, BC["ones"][0]:BC["ones"][0] + 128] = 1.0
    cba[:, BC["ones_lo"][0]:BC["ones_lo"][0] + 64] = 1.0
    cba[:, BC["ones_hi"][0] + 64:BC["ones_hi"][0] + 128] = 1.0
    cba[:, BC["su"][0]:BC["su"][0] + 128] = (u > t).astype(f32)
    cba[:, BC["m01"][0]:BC["m01"][0] + 128] = ((u <= t) & (u // 64 == t // 64)).astype(f32)
    cba[:, BC["m01s"][0]:BC["m01s"][0] + 32] = m01s
    gt = np.zeros((2, 128, GT_W), f32)
    gw = np.zeros((2, 128, GW_W), f32)
    for l in range(2):
        gt[l, :, 0:512] = inp["gm_ln_g"][l][None, :]
        gt[l, :, 512:1024] = inp["gm_ln_b"][l][None, :]
        gt[l, :, 1024:1536] = inp["gm_b_s"][l].reshape(1, 512)
        gt[l, :, 1536:1664] = np.concatenate([np.tile(inp["gm_b_s"][l][g, :16], 2) for g in range(4)])[None, :]
        for g in range(4):
            WT = inp["gm_w_s"][l][g].T
            gw[l, :, g * 128:(g + 1) * 128] = WT
            blk = np.zeros((32, 32), f32)
            blk[0:16, 0:16] = WT[0:16, 0:16]
            blk[16:32, 16:32] = WT[0:16, 0:16]
            gw[l, 0:32, 512 + g * 32:512 + (g + 1) * 32] = blk
    half = 8
    inv_freq = (500000.0 ** (-np.arange(half, dtype=np.float32) * (2.0 / 16))).astype(f32)
    rope = np.zeros((2, 2, 128, T), f32)
    for b in range(2):
        pos = np.concatenate([np.arange(b * 1024, (b + 1) * 1024), 4096 + np.arange(16), 4096 + np.arange(16)]).astype(f32)
        ang = pos[None, :] * inv_freq[:, None]
        cos = np.cos(ang).astype(f32)
        sin = np.sin(ang).astype(f32)
        for pp in range(128):
            d = pp % 64
            if d < 8:
                rope[b, 0, pp] = cos[d]
                rope[b, 1, pp] = -sin[d]
            elif d < 16:
                rope[b, 0, pp] = cos[d - 8]
                rope[b, 1, pp] = sin[d - 8]
            else:
                rope[b, 0, pp] = 1.0
    in_maps = []
    for core in cores:
        xT = np.zeros((2, 128, 8, T), f32)
        for b in range(2):
            tok = np.concatenate([inp["x_prompt"][core, b * 1024:(b + 1) * 1024],
                                  inp["x_sample"][4 * core + 2 * b], inp["x_sample"][4 * core + 2 * b + 1]], axis=0)
            xT[b] = tok.reshape(T, 8, 128).transpose(2, 1, 0)
        cs = np.concatenate([inp["c_prompt"][core][None, :], inp["c_sample"][4 * core:4 * core + 4]], axis=0)
        cT = cs.reshape(5, 8, 128).transpose(2, 1, 0)
        kc = np.zeros((4, 2, 128, 2, 128), f32)
        vc = np.zeros((4, 2, 128, 2, 2, 128), f32)
        hs = np.zeros((4, 2, 128, 1024), f32)
        cst = np.zeros((4, 2, 128, 12, 3), f32)
        for s in range(4):
            sb_ = 4 * core + s
            for l in range(2):
                ck = inp["cache_attn_k"][l, sb_]
                cv = inp["cache_attn_v"][l, sb_]
                for g in range(2):
                    kt = ck[:, g, :].T
                    kc[s, l, 0:64, g, :] = kt
                    kc[s, l, 64:128, g, :] = kt
                    vc[s, l, :, g, 0, 0:64] = cv[:, g, :]
                    vc[s, l, :, g, 1, 64:128] = cv[:, g, :]
                hs[s, l] = inp["state_ssm"][l, sb_].reshape(1024, 128).T
                cst[s, l] = inp["state_conv"][l, sb_].reshape(3, 12, 128).transpose(2, 1, 0)
        in_maps.append({"xT": np.ascontiguousarray(xT), "cT": np.ascontiguousarray(cT), "wst": wst, "cf": cfa, "cb": cba,
                        "rope": rope, "gt": gt, "gw": gw, "kc": kc, "vc": vc, "hs": hs, "cs": cst})
    return in_maps


def _assemble(R, cores=range(8)):
    f32 = np.float32
    y_prompt = np.zeros((8, 2048, 1024), f32)
    y_sample = np.zeros((32, 16, 1024), f32)
    nkp = np.zeros((2, 8, 128, 2, 64), f32)
    nvp = np.zeros((2, 8, 128, 2, 64), f32)
    nsp = np.zeros((2, 8, 16, 64, 128), f32)
    ncp = np.zeros((2, 8, 3, 1536), f32)
    nks = np.zeros((2, 32, 16, 2, 64), f32)
    nvs = np.zeros((2, 32, 16, 2, 64), f32)
    nss = np.zeros((2, 32, 16, 64, 128), f32)
    ncs = np.zeros((2, 32, 3, 1536), f32)
    ngv = np.zeros((2, 32, 16, 512), f32)
    for ci, core in enumerate(cores):
        r = R[ci]
        for b in range(2):
            tok = r["yT"][b].transpose(2, 1, 0).reshape(T, 1024)
            y_prompt[core, b * 1024:(b + 1) * 1024] = tok[:TP]
            y_sample[4 * core + 2 * b] = tok[TP:TP + 16]
            y_sample[4 * core + 2 * b + 1] = tok[TP + 16:T]
        for l in range(2):
            nkp[l, core] = r["kp"][l][0:64].transpose(2, 1, 0)
            nvp[l, core] = r["vp"][l].reshape(128, 2, 64)
            nsp[l, core] = r["hp"][l].T.reshape(16, 64, 128)
            ncp[l, core] = r["cpo"][l].transpose(2, 1, 0).reshape(3, 1536)
            for b in range(2):
                kk = r["ks"][l, b][0:64].transpose(2, 1, 0)
                vv = r["vs"][l, b].reshape(32, 2, 64)
                gg = r["gvo"][l, b]
                for s in range(2):
                    sb_ = 4 * core + 2 * b + s
                    nks[l, sb_] = kk[16 * s:16 * (s + 1)]
                    nvs[l, sb_] = vv[16 * s:16 * (s + 1)]
                    ngv[l, sb_] = gg[16 * s:16 * (s + 1)]
            for s in range(4):
                sb_ = 4 * core + s
                nss[l, sb_] = r["hso"][l, s].T.reshape(16, 64, 128)
                ncs[l, sb_] = r["cso"][l, s].transpose(2, 1, 0).reshape(3, 1536)
    return (y_prompt, y_sample, nkp, nvp, nsp, ncp, nks, nvs, nss, ncs, ngv)


def kernel(**inp):
    inp = {k: np.asarray(v) for k, v in inp.items()}
    nc, keys = _get_program()
    in_maps = _prep(inp, keys)
    res = run_bass_kernel_spmd(nc, in_maps, core_ids=list(range(8)))
    return _assemble(res.results)
```

```python
import math
from contextlib import ExitStack

import numpy as np
import concourse.bass as bass
import concourse.mybir as mybir
from concourse.bass_utils import run_bass_kernel_spmd

F32 = mybir.dt.float32
BF16 = mybir.dt.bfloat16
ALU = mybir.AluOpType
AF = mybir.ActivationFunctionType

ENGS = ("pe", "dve", "act", "pool", "sp")


class Res:
    __slots__ = ("name", "writers", "readers")

    def __init__(self, name=""):
        self.name = name
        self.writers = []
        self.readers = []


class DmaSlot:
    def __init__(self, sem):
        self.sem = sem
        self.count = 0


class Op:
    __slots__ = ("eng", "fn", "deps", "slot", "signal", "count", "dma_deps")

    def __init__(self, eng, fn):
        self.eng = eng
        self.fn = fn
        self.deps = []
        self.slot = None
        self.signal = False
        self.count = 0
        self.dma_deps = []


class Prog:
    def __init__(self):
        self.ops = []
        self.last = {}
        self.bar_slots = []
        self.pool_fence = []

    def _dep(self, op, d):
        if d is op:
            return
        if d.slot is not None:
            op.dma_deps.append((d.slot, d.slot.count * 16))
        elif d.fn is not None:
            op.deps.append(d)

    def _track(self, op, reads, writes, deps):
        for r in reads:
            for d in r.writers:
                self._dep(op, d)
            r.readers.append(op)
        for r in writes:
            if r.readers:
                for d in r.readers:
                    self._dep(op, d)
                for d in r.writers:
                    self._dep(op, d)
                r.writers = [op]
                r.readers = []
            else:
                if r.writers and not (r.writers[-1].eng == "pe" and op.eng == "pe" and r.writers[-1].slot is None):
                    self._dep(op, r.writers[-1])
                r.writers.append(op)
                if len(r.writers) > 64:
                    r.writers = r.writers[-64:]
        for d in deps:
            self._dep(op, d)

    def add(self, eng, fn, reads=(), writes=(), deps=()):
        op = Op(eng, fn)
        self._track(op, reads, writes, deps)
        self.ops.append(op)
        if fn is not None:
            self.last[eng] = op
        return op

    def dma(self, queue, slot, out, in_, reads=(), writes=(), deps=()):
        def fn(e, out=out, in_=in_):
            return e.dma_start(out=out, in_=in_)
        op = Op(queue, fn)
        self._track(op, reads, writes, deps)
        slot.count += 1
        op.slot = slot
        self.ops.append(op)
        return op

    def barrier(self, engs=("pe", "dve", "act", "sp")):
        last = dict(self.last)
        self.pool_fence = [o for k, o in last.items() if k in ("pe", "dve", "act")]
        for e in engs:
            ds = [o for k, o in last.items() if k != e and k in ("pe", "dve", "act", "pool")]
            op = self.add(e, None, deps=ds)
            for sl in self.bar_slots:
                if sl.count:
                    op.dma_deps.append((sl, sl.count * 16))

    def emit(self, sems, block):
        for i, op in enumerate(self.ops):
            op.count = i
        for op in self.ops:
            best = {}
            for d in op.deps:
                if d.eng not in best or d.count > best[d.eng].count:
                    best[d.eng] = d
            op.deps = list(best.values())
            for d in op.deps:
                d.signal = True
        for op in self.ops:
            op.count = 0
        cnt = {e: 0 for e in ENGS}
        per = {e: [] for e in ENGS}
        for op in self.ops:
            if op.slot is None and op.signal and op.fn is not None:
                cnt[op.eng] += 1
                op.count = cnt[op.eng]
            per[op.eng].append(op)
        engobj = {"pe": "tensor", "dve": "vector", "act": "scalar", "pool": "gpsimd", "sp": "sync"}

        def make(ename):
            oplist = per[ename]

            def body(e):
                waited = {}
                waited_dma = {}
                for op in oplist:
                    need = {}
                    for d in op.deps:
                        if d.count > need.get(d.eng, 0):
                            need[d.eng] = d.count
                    for pe_, c in need.items():
                        if pe_ == ename and ename == "pe":
                            continue
                        if waited.get(pe_, 0) < c:
                            e.wait_ge(sems[pe_], c)
                            waited[pe_] = c
                    for slot, v in op.dma_deps:
                        if waited_dma.get(id(slot), 0) < v:
                            e.wait_ge(slot.sem, v)
                            waited_dma[id(slot)] = v
                    if op.fn is None:
                        continue
                    ins = op.fn(e)
                    if op.slot is not None:
                        ins.then_inc(op.slot.sem, 16)
                    elif op.signal:
                        ins.then_inc(sems[ename], 1)
            return body

        for ename in ENGS:
            if per[ename]:
                getattr(block, engobj[ename])(make(ename))


D = 1024
TP = 1024
TS = 32
T = TP + TS
NT = [(0, 352), (352, 352), (704, 352)]
NB = 6
EPS = 1e-6
NWCH = 404
AW = 16896

O_Q, O_K, O_V, O_Z, O_X, O_B, O_C, O_DT, O_GU, O_GV, O_G = 0, 512, 640, 768, 1792, 2816, 3072, 3328, 3344, 3856, 4368

LC = {}
_o = 0
for _n, _w in (("gmix", 8), ("gff", 8), ("convw", 48), ("convb", 12), ("normw", 8), ("dskip", 8), ("sink", 4),
               ("bada", 48), ("dtb", 16), ("alog", 16)):
    LC[_n] = (_o, _w)
    _o += _w
LCW = _o
GC = {}
_o = 2 * LCW
for _n, _w in (("gfin", 8), ("m01", 128), ("su", 128), ("onesf", 128), ("m01s", 32), ("mk", 2), ("m01g", 128), ("mkp", 2)):
    GC[_n] = (_o, _w)
    _o += _w
NCF = _o
BC = {}
_o = 0
for _n, _w in (("ident", 128), ("ones", 128), ("ones_lo", 128), ("ones_hi", 128), ("su", 128), ("m01", 128), ("m01s", 32)):
    BC[_n] = (_o, _w)
    _o += _w
NCB = _o
GT_W = 512 + 512 + 512 + 128
GW_W = 4 * 128 + 4 * 32


def _rot_src(d):
    if d < 8:
        return d + 8
    if d < 16:
        return d - 8
    return -1


class Builder:
    def __init__(self, limit=100000):
        self.keys = []
        self.keyidx = {}
        self.limit = limit
        import os
        self.sub = int(os.environ.get("KSUB", "100"))
        self.ss = int(os.environ.get("KSS", "100"))

    def wkey(self, key):
        if key not in self.keyidx:
            self.keyidx[key] = len(self.keys)
            self.keys.append(key)
        return self.keyidx[key]

    def build(self, nw_chunks):
        nc = bass.Bass("TRN2", target_bir_lowering=False)
        self.nc = nc
        P = Prog()
        self.P = P

        def din(name, shape):
            return nc.dram_tensor(name, shape, F32, kind="ExternalInput").ap()

        def dout(name, shape):
            return nc.dram_tensor(name, shape, F32, kind="ExternalOutput").ap()

        xT_d = din("xT", [2, 128, 8, T])
        cT_d = din("cT", [128, 8, 5])
        wst_d = din("wst", [nw_chunks, 128, 8, 128])
        cf_d = din("cf", [128, NCF])
        cb_d = din("cb", [128, NCB])
        rope_d = din("rope", [2, 2, 128, T])
        gt_d = din("gt", [2, 128, GT_W])
        gw_d = din("gw", [2, 128, GW_W])
        kc_d = din("kc", [4, 2, 128, 2, 128])
        vc_d = din("vc", [4, 2, 128, 2, 2, 128])
        hs_d = din("hs", [4, 2, 128, 1024])
        cs_d = din("cs", [4, 2, 128, 12, 3])

        yT_o = dout("yT", [2, 128, 8, T])
        kp_o = dout("kp", [2, 128, 2, 128])
        vp_o = dout("vp", [2, 128, 128])
        ks_o = dout("ks", [2, 2, 128, 2, 32])
        vs_o = dout("vs", [2, 2, 32, 128])
        hp_o = dout("hp", [2, 128, 1024])
        hso_o = dout("hso", [2, 4, 128, 1024])
        cp_o = dout("cpo", [2, 128, 12, 3])
        cso_o = dout("cso", [2, 4, 128, 12, 3])
        gv_o = dout("gvo", [2, 2, 32, 512])

        with ExitStack() as es:
            E = es.enter_context

            def sb(name, shape, dt):
                return E(nc.sbuf_tensor(name, shape, dt))

            X = [sb("X0", [128, 8, T], F32), sb("X1", [128, 8, T], F32)]
            rX = [[Res() for _ in range(8)] for _ in range(2)]
            H = sb("H", [128, 8, T], BF16)
            rHt = [Res() for _ in NT]

            class _HRes:
                def __call__(self, t0, tn):
                    return [rHt[i] for i, (a, n) in enumerate(NT) if a < t0 + tn and t0 < a + n]
            rHof = _HRes()
            MG = sb("MG", [128, 8, T], BF16)
            rMG = Res()
            BR = sb("BR", [128, 8, T], BF16)
            rBR = Res()
            WB = [sb(f"wb{i}", [128, 8, 128], BF16) for i in range(NB)]
            rWB = [Res() for _ in range(NB)]
            ARENA = sb("arena", [128, AW], F32)
            CF = sb("cf_s", [128, NCF], F32)
            rCF = Res()
            CB = sb("cb_s", [128, NCB], BF16)
            rCB = Res()
            MOD = sb("mod", [128, 2, 48, 5], F32)
            rMOD = Res()
            AA = sb("aa", [128, 2, 2, 8, 5], F32)
            rAA = Res()
            ESK = sb("esk", [128, 2, 4], F32)
            rESK = Res()
            AROW = sb("arow", [128, 2, 16], F32)
            rAROW = Res()
            KTC = sb("ktc", [128, 2, 128], BF16)
            rKTC = Res()
            VLC = sb("vlc", [128, 4, 128], BF16)
            rVLC = Res()
            CCAR = sb("ccar", [128, 12, 3], F32)
            rCCAR = Res()
            HST = sb("hst", [128, 2, 512], F32)
            rHST = [Res(), Res()]
            CTs = sb("cts", [128, 8, 5], F32)
            CTb = sb("ctb", [128, 8, 5], BF16)
            rCT = Res()
            PS = [E(nc.psum_tensor(f"ps{i}", [128, 512], F32)) for i in range(8)]
            rPS = [Res() for _ in range(8)]

            sems = {e: E(nc.semaphore(f"s_{e}")) for e in ("pe", "dve", "act", "pool")}
            wsl = [DmaSlot(E(nc.semaphore(f"w{i}"))) for i in range(NB)]
            ldsl = DmaSlot(E(nc.semaphore("ld")))
            ld2 = DmaSlot(E(nc.semaphore("ld2")))
            osl = DmaSlot(E(nc.semaphore("os")))
            block = E(nc.Block())
            P.bar_slots = [osl]

            def pool_arena_dma(out, in_, Wr):
                return P.dma("pool", ld2, out, in_, writes=Wr, deps=[o for k, o in P.last.items() if k in ("pe", "dve", "act")])

            def cf(name, l=None):
                if l is None:
                    o, w = GC[name]
                else:
                    o, w = LC[name]
                    o += l * LCW
                return CF[:, o:o + w]

            def cb(name):
                o, w = BC[name]
                return CB[:, o:o + w]

            arena_top = [0]

            def aalloc(shape, dt):
                n = int(np.prod(shape))
                nbytes = n * (4 if dt == F32 else 2)
                w = (nbytes + 3) // 4
                w = (w + 7) // 8 * 8
                a = ARENA[:, arena_top[0]:arena_top[0] + w]
                arena_top[0] += w
                assert arena_top[0] <= AW, f"arena overflow {arena_top[0]}"
                v = a if dt == F32 else a.bitcast(dt)
                v = v[:, :n]
                if len(shape) > 1:
                    names = "abcdef"[:len(shape)]
                    pat = f"p ({' '.join(names)}) -> p {' '.join(names)}"
                    v = v.rearrange(pat, **{nm: s for nm, s in zip(names, shape)})
                return v, Res()

            def areset():
                P.barrier()
                arena_top[0] = 0

            wcount = [0]

            def wnext(key):
                idx = self.wkey(key)
                s = wcount[0] % NB
                wcount[0] += 1
                P.dma("pool", wsl[s], WB[s][:], wst_d[idx], writes=[rWB[s]])
                return WB[s], rWB[s]

            def mm(out, lhsT, rhs, start, stop, R, Wr):
                return P.add("pe", lambda e: e.matmul(out, lhsT=lhsT, rhs=rhs, start=start, stop=stop), reads=R, writes=Wr)

            def tp(out, in_, ident, R, Wr):
                return P.add("pe", lambda e: e.transpose(out, in_, ident), reads=R, writes=Wr)

            def act(out, in_, func, R, Wr, bias=None, scale=None):
                kw = {}
                if bias is not None:
                    kw["bias"] = bias
                if scale is not None:
                    kw["scale"] = scale
                return P.add("act", lambda e: e.activation(out=out, in_=in_, func=func, **kw), reads=R, writes=Wr)

            def tt(eng, out, a, b, op, R, Wr):
                return P.add(eng, lambda e: e.tensor_tensor(out=out, in0=a, in1=b, op=op), reads=R, writes=Wr)

            def ts(eng, out, a, s1, s2, op0, op1, R, Wr):
                if op1 is None:
                    return P.add(eng, lambda e: e.tensor_scalar(out=out, in0=a, scalar1=s1, scalar2=None, op0=op0), reads=R, writes=Wr)
                return P.add(eng, lambda e: e.tensor_scalar(out=out, in0=a, scalar1=s1, scalar2=s2, op0=op0, op1=op1), reads=R, writes=Wr)

            def stt(eng, out, a, s, b, op0, op1, R, Wr):
                return P.add(eng, lambda e: e.scalar_tensor_tensor(out=out, in0=a, scalar=s, in1=b, op0=op0, op1=op1), reads=R, writes=Wr)

            def cpy(eng, out, in_, R, Wr):
                if eng == "act":
                    return P.add("act", lambda e: e.activation(out=out, in_=in_, func=AF.Copy), reads=R, writes=Wr)
                return P.add(eng, lambda e: e.tensor_copy(out=out, in_=in_), reads=R, writes=Wr)

            def rsqrt_inplace(ap, r):
                act(ap, ap, AF.Ln, [r], [r])
                act(ap, ap, AF.Exp, [r], [r], scale=-0.5)

            def recip(out, in_, R, Wr):
                return P.add("dve", lambda e: e.reciprocal(out=out, in_=in_), reads=R, writes=Wr)

            def memset(eng, ap, val, Wr):
                return P.add(eng, lambda e: e.memset(ap, val), writes=Wr)

            def odma(out, in_, R):
                return P.dma("sp", osl, out, in_, reads=R)

            def rs_of(rows):
                return slice(0, rows)

            bankrr = [0]

            def nextbank(banks):
                b = banks[bankrr[0] % len(banks)]
                bankrr[0] += 1
                return b

            def groups(b):
                return [(0, TP, 0), (TP, 16, 1 + 2 * b), (TP + 16, 16, 2 + 2 * b)]

            P.dma("sp", ldsl, CF[:], cf_d, writes=[rCF])
            P.dma("pool", ld2, CB[:], cb_d, writes=[rCB])
            P.dma("sp", ldsl, CTs[:], cT_d, writes=[rCT])
            for b in range(2):
                for c in range(8):
                    P.dma("sp", ldsl, X[b][:, c, :], xT_d[b, :, c, :], writes=[rX[b][c]])
            act(CTb[:], CTs[:], AF.Silu, [rCT], [rCT])
            rMODl = [Res(), Res()]
            rAAl = [Res(), Res()]

            def ada_chunk(l, c, bank):
                wb, rw = wnext(("ada", l, c))
                for k in range(8):
                    mm(PS[bank][:, c * 5:(c + 1) * 5], wb[:, k, :], CTb[:, k, :], k == 0, k == 7, [rw, rCT], [rPS[bank]])

            def ada_finish(l, bank):
                bada = cf("bada", l)
                tt("dve", MOD[:, l, :, :], PS[bank][:, 0:240].rearrange("p (c s) -> p c s", c=48),
                   bada.unsqueeze(2).to_broadcast([128, 48, 5]), ALU.add, [rPS[bank], rCF], [rMODl[l]])
                for wi, (gname, mi) in enumerate((("gmix", 1), ("gff", 4))):
                    stt("dve", AA[:, l, wi, :, :], MOD[:, l, mi * 8:(mi + 1) * 8, :], 1.0,
                        cf(gname, l).unsqueeze(2).to_broadcast([128, 8, 5]), ALU.add, ALU.mult, [rMODl[l], rCF], [rAAl[l]])

            for c in range(48):
                ada_chunk(0, c, 0)
            ada_finish(0, 0)
            for l in range(2):
                act(ESK[:, l, :], cf("sink", l), AF.Exp, [rCF], [rESK])
                act(AROW[:, l, :], cf("alog", l), AF.Exp, [rCF], [rAROW])
                ts("dve", AROW[:, l, :], AROW[:, l, :], -1.0, None, ALU.mult, None, [rAROW], [rAROW])
            ada_todo = list(range(48))

            def modcol(l, mi, c, s):
                return MOD[:, l, mi * 8 + c, s:s + 1]

            def rms_rstd(b, dst, rdst):
                banks = [0, 1, 2]
                sqs = [aalloc([T], BF16) for _ in range(2)]
                for c in range(8):
                    sq, rsq = sqs[c % 2]
                    act(sq[:, :], X[b][:, c, :], AF.Square, [rX[b][c]], [rsq])
                    for ti, (t0, tn) in enumerate(NT):
                        mm(PS[banks[ti]][:, :tn], cb("ones"), sq[:, t0:t0 + tn], c == 0, c == 7, [rsq, rCB], [rPS[banks[ti]]])
                for ti, (t0, tn) in enumerate(NT):
                    ts("dve", dst[:, t0:t0 + tn], PS[banks[ti]][:, :tn], 1.0 / D, EPS, ALU.mult, ALU.add, [rPS[banks[ti]]], [rdst])
                rsqrt_inplace(dst[:, :], rdst)

            def norm_mod(b, l, wi, mi_shift):
                rstd, _ = aalloc([T], F32)
                rr = [Res() for _ in NT]
                sqs = [aalloc([352], BF16) for _ in range(4)]
                tmps = [aalloc([352], F32) for _ in range(4)]
                banks = [0, 1, 2]
                for ti, (t0, tn) in enumerate(NT):
                    bank = banks[ti]
                    for c in range(8):
                        sq, rsq = sqs[c % 4]
                        act(sq[:, :tn], X[b][:, c, t0:t0 + tn], AF.Square, [rX[b][c]], [rsq])
                        mm(PS[bank][:, :tn], cb("ones"), sq[:, :tn], c == 0, c == 7, [rsq, rCB], [rPS[bank]])
                    ts("dve", rstd[:, t0:t0 + tn], PS[bank][:, :tn], 1.0 / D, EPS, ALU.mult, ALU.add, [rPS[bank]], [rr[ti]])
                    rsqrt_inplace(rstd[:, t0:t0 + tn], rr[ti])
                    for c in range(8):
                        tmp, rtmp = tmps[c % 4]
                        tt("dve", tmp[:, :tn], X[b][:, c, t0:t0 + tn], rstd[:, t0:t0 + tn], ALU.mult, [rX[b][c], rr[ti]], [rtmp])
                        for (g0, gn, sq_) in groups(b):
                            lo, hi = max(g0, t0), min(g0 + gn, t0 + tn)
                            if lo >= hi:
                                continue
                            act(H[:, c, lo:hi], tmp[:, lo - t0:hi - t0], AF.Identity, [rtmp, rAAl[l], rMODl[l]], [rHt[ti]],
                                bias=modcol(l, mi_shift, c, sq_), scale=AA[:, l, wi, c, sq_:sq_ + 1])

            def proj_fm(wb, rw, kn, koff, src, rsrc, banks, evac):
                for ti, (t0, tn) in enumerate(NT):
                    bank = nextbank(banks)
                    rs_list = rsrc(t0, tn) if callable(rsrc) else [rsrc]
                    for k in range(kn):
                        mm(PS[bank][:, :tn], wb[:, koff + k, :], src[:, k, t0:t0 + tn], k == 0, k == kn - 1, [rw] + rs_list, [rPS[bank]])
                    evac(bank, ti, t0, tn)

            def merge_branch(l, bi, kn, packed):
                areset()
                gs = [aalloc([512], F32) for _ in range(2)]
                tms = [aalloc([512], F32) for _ in range(2)]
                cnt = 0
                for c in range(8):
                    if packed:
                        if c % 2 == 0:
                            wo, rwo = wnext((("ao" if bi == 0 else "go"), l, c // 2))
                        koff = (c % 2) * 4
                    else:
                        wo, rwo = wnext(("so", l, c))
                        koff = 0
                    wg, rwg = wnext(("in", l, "gate", bi, c))
                    for ti, (t0, tn) in enumerate(NT):
                        b1 = nextbank([0, 1, 2, 3, 4, 5, 6, 7])
                        b2 = nextbank([0, 1, 2, 3, 4, 5, 6, 7])
                        for k in range(8):
                            mm(PS[b1][:, :tn], wg[:, k, :], H[:, k, t0:t0 + tn], k == 0, k == 7, [rwg] + rHof(t0, tn), [rPS[b1]])
                        for k in range(kn):
                            mm(PS[b2][:, :tn], wo[:, koff + k, :], BR[:, k, t0:t0 + tn], k == 0, k == kn - 1, [rwo, rBR], [rPS[b2]])
                        g, rg = gs[cnt % 2]
                        tm, rtm = tms[cnt % 2]
                        cnt += 1
                        act(g[:, :tn], PS[b1][:, :tn], AF.Sigmoid, [rPS[b1]], [rg])
                        if bi == 0:
                            tt("dve", MG[:, c, t0:t0 + tn], PS[b2][:, :tn], g[:, :tn], ALU.mult, [rPS[b2], rg], [rMG])
                        else:
                            tt("dve", tm[:, :tn], PS[b2][:, :tn], g[:, :tn], ALU.mult, [rPS[b2], rg], [rtm])
                            tt("dve", MG[:, c, t0:t0 + tn], MG[:, c, t0:t0 + tn], tm[:, :tn], ALU.add, [rtm, rMG], [rMG])

            def attention(b, l):
                areset()
                QT, rQT = aalloc([4, 2, T], BF16)
                KT, rKT = aalloc([2, 128 + T], BF16)
                VLH, rVLH = aalloc([10, 4, 128], BF16)
                COS, rCOS = aalloc([T], F32)
                SIN, rSIN = aalloc([T], F32)
                KF, rKF = aalloc([2, 128 + TS], F32)
                VF, rVF = aalloc([2, 128], F32)
                t1s = [aalloc([512], F32) for _ in range(2)]
                t2s = [aalloc([512], F32) for _ in range(2)]
                PTp = [aalloc([4, 128], BF16) for _ in range(2)]
                PTc = [aalloc([4, 128], BF16) for _ in range(2)]
                RD = [aalloc([128], F32) for _ in range(2)]
                KCs, rKCs = aalloc([2, 2, 128], BF16)
                VCs, rVCs = aalloc([2, 2, 2, 128], BF16)
                PTsc, rPTsc = aalloc([4, 16], BF16)
                PTsn, rPTsn = aalloc([4, 16], BF16)
                P.dma("sp", ldsl, COS[:, :], rope_d[b, 0], writes=[rCOS])
                P.dma("sp", ldsl, SIN[:, :], rope_d[b, 1], writes=[rSIN])
                for s in range(2):
                    pool_arena_dma(KCs[:, s, :, :], kc_d[2 * b + s, l], [rKCs])
                    pool_arena_dma(VCs[:, s, :, :, :], vc_d[2 * b + s, l], [rVCs])
                memset("dve", VLH[:, :, :, :], 0.0, [rVLH])
                memset("dve", QT[:, :, :, :], 0.0, [rQT])
                for i in range(2):
                    memset("dve", PTp[i][0][:, :, :], 0.0, [PTp[i][1]])
                    memset("dve", PTc[i][0][:, :, :], 0.0, [PTc[i][1]])
                if b == 1:
                    cpy("dve", KT[:, :, 0:128], KTC[:, :, :], [rKTC], [rKT])
                    cpy("dve", VLH[:, 0, :, :], VLC[:, :, :], [rVLC], [rVLH])
                else:
                    memset("dve", KT[:, :, 0:128], 0.0, [rKT])

                if self.sub < 1:
                    return
                def rope_proj(kind, kindr, j, dst_fn):
                    wa, rwa = wnext(("in", l, kind, j))
                    wr, rwr = wnext(("in", l, kindr, j))
                    for ti, (t0, tn) in enumerate(NT):
                        b1 = nextbank([0, 1, 2, 3, 4, 5, 6, 7])
                        b2 = nextbank([0, 1, 2, 3, 4, 5, 6, 7])
                        for k in range(8):
                            mm(PS[b1][:, :tn], wa[:, k, :], H[:, k, t0:t0 + tn], k == 0, k == 7, [rwa] + rHof(t0, tn), [rPS[b1]])
                        for k in range(8):
                            mm(PS[b2][:, :tn], wr[:, k, :], H[:, k, t0:t0 + tn], k == 0, k == 7, [rwr] + rHof(t0, tn), [rPS[b2]])
                        t1, rt1 = t1s[ti % 2]
                        t2, rt2 = t2s[ti % 2]
                        tt("dve", t1[:, :tn], PS[b1][:, :tn], COS[:, t0:t0 + tn], ALU.mult, [rPS[b1], rCOS], [rt1])
                        tt("dve", t2[:, :tn], PS[b2][:, :tn], SIN[:, t0:t0 + tn], ALU.mult, [rPS[b2], rSIN], [rt2])
                        dst_fn(ti, t0, tn, t1, rt1, t2, rt2)

                for j in range(4):
                    def dq(ti, t0, tn, t1, rt1, t2, rt2, j=j):
                        tt("dve", QT[0:64, j, 0, t0:t0 + tn], t1[0:64, :tn], t2[0:64, :tn], ALU.add, [rt1, rt2], [rQT])
                        tt("dve", QT[64:128, j, 1, t0:t0 + tn], t1[64:128, :tn], t2[64:128, :tn], ALU.add, [rt1, rt2], [rQT])
                    rope_proj("q", "qr", j, dq)
                for g in range(2):
                    def dk(ti, t0, tn, t1, rt1, t2, rt2, g=g):
                        tt("dve", KT[:, g, 128 + t0:128 + t0 + tn], t1[:, :tn], t2[:, :tn], ALU.add, [rt1, rt2], [rKT])
                        if b == 1:
                            lo, hi = max(t0, TP - 128), min(t0 + tn, TP)
                            if lo < hi:
                                tt("dve", KF[:, g, lo - (TP - 128):hi - (TP - 128)], t1[:, lo - t0:hi - t0], t2[:, lo - t0:hi - t0], ALU.add,
                                   [rt1, rt2], [rKF])
                        lo, hi = max(t0, TP), min(t0 + tn, T)
                        if lo < hi:
                            tt("dve", KF[:, g, 128 + lo - TP:128 + hi - TP], t1[:, lo - t0:hi - t0], t2[:, lo - t0:hi - t0], ALU.add,
                               [rt1, rt2], [rKF])
                    rope_proj("k", "kr", g, dk)
                if self.sub < 2:
                    return
                wv, rwv = wnext(("in", l, "v"))
                for i in range(9):
                    rows = 128 if i < 8 else TS
                    t0 = i * 128
                    bank = nextbank([0, 1, 2, 3, 4, 5, 6, 7])
                    for k in range(8):
                        mm(PS[bank][:rows, 0:128], H[:, k, t0:t0 + rows], wv[:, k, :], k == 0, k == 7, [rwv] + rHof(t0, rows), [rPS[bank]])
                    for g in range(2):
                        cpy("act", VLH[:rows, i + 1, 2 * g, 0:64], PS[bank][:rows, g * 64:(g + 1) * 64], [rPS[bank]], [rVLH])
                        cpy("act", VLH[:rows, i + 1, 2 * g + 1, 64:128], PS[bank][:rows, g * 64:(g + 1) * 64], [rPS[bank]], [rVLH])
                    if i == 7 and b == 1:
                        cpy("dve", VF[:, 0, :], PS[bank][:, 0:128], [rPS[bank]], [rVF])
                    if i == 8:
                        cpy("dve", VF[:rows, 1, :], PS[bank][:rows, 0:128], [rPS[bank]], [rVF])
                if self.sub < 3:
                    return
                if b == 1:
                    odma(kp_o[l], KF[:, :, 0:128], [rKF])
                    odma(vp_o[l], VF[:, 0, :], [rVF])
                else:
                    cpy("dve", KTC[:, :, :], KT[:, :, 128 + 896:128 + 1024], [rKT], [rKTC])
                    cpy("dve", VLC[:, :, :], VLH[:, 8, :, :], [rVLH], [rVLC])
                odma(ks_o[l, b], KF[:, :, 128:128 + TS], [rKF])
                odma(vs_o[l, b], VF[:TS, 1, :], [rVF])

                if self.sub < 4:
                    return
                ptstore = {}

                def stageQ(i, g, it):
                    have_prev = not (b == 0 and i == 0)
                    qc = slice(i * 128, (i + 1) * 128)
                    pts = {}
                    for kt in ((0, 1) if have_prev else (1,)):
                        bank = nextbank([0, 1, 2, 3])
                        kcols = slice(i * 128, (i + 1) * 128) if kt == 0 else slice(128 + i * 128, 128 + (i + 1) * 128)
                        for hh in range(4):
                            jj = 2 * g + hh // 2
                            half = hh % 2
                            mm(PS[bank][:, hh * 128:(hh + 1) * 128], KT[:, g, kcols], QT[:, jj, half, qc], True, True,
                               [rKT, rQT], [rPS[bank]])
                        psv = PS[bank][:, :].rearrange("p (h q) -> p h q", h=4)
                        if kt == 0:
                            pt, rpt = PTp[it % 2]
                            act(pt[:, :, 0:64], psv[:, :, 0:64], AF.Exp, [rPS[bank]], [rpt], scale=0.125)
                            act(pt[64:128, :, 64:128], psv[64:128, :, 64:128], AF.Exp, [rPS[bank]], [rpt], scale=0.125)
                        else:
                            pt, rpt = PTc[it % 2]
                            act(pt[0:64, :, 0:64], psv[0:64, :, 0:64], AF.Exp, [rPS[bank]], [rpt], scale=0.125)
                            act(pt[:, :, 64:128], psv[:, :, 64:128], AF.Exp, [rPS[bank]], [rpt], scale=0.125)
                        pts[(g, kt)] = (pt, rpt)
                    ptstore[(i, g)] = pts

                def stageP(i, g):
                    have_prev = not (b == 0 and i == 0)
                    qc = slice(i * 128, (i + 1) * 128)
                    pts = ptstore.pop((i, g))
                    for jl in range(2):
                        jj = 2 * g + jl
                        bo = nextbank([4, 5, 6, 7])
                        kts = (0, 1) if have_prev else (1,)
                        n_mm = 2 * len(kts)
                        for which in range(2):
                            col = slice(which * 128, (which + 1) * 128)
                            m = 0
                            for kt in kts:
                                pt, rpt = pts[(g, kt)]
                                vt = i if kt == 0 else i + 1
                                for half in range(2):
                                    hh = 2 * jl + half
                                    if which == 0:
                                        lh = VLH[:, vt, 2 * g + half, :]
                                    else:
                                        lh = cb("ones_lo") if half == 0 else cb("ones_hi")
                                    mm(PS[bo][:, col], lh, pt[:, hh, :], m == 0, m == n_mm - 1, [rVLH, rpt, rCB], [rPS[bo]])
                                    m += 1
                        rd, rrd = RD[(i * 4 + jj) % 2]
                        ts("dve", rd[:, :], PS[bo][:, 128:256], ESK[:, l, jj:jj + 1], None, ALU.add, None, [rPS[bo], rESK], [rrd])
                        recip(rd[:, :], rd[:, :], [rrd], [rrd])
                        tt("dve", BR[:, jj, qc], PS[bo][:, 0:128], rd[:, :], ALU.mult, [rPS[bo], rrd], [rBR])

                seq = [(i, g) for i in range(8) for g in range(2)]
                stageQ(seq[0][0], seq[0][1], 0)
                for n_, (i, g) in enumerate(seq):
                    if n_ + 1 < len(seq):
                        stageQ(seq[n_ + 1][0], seq[n_ + 1][1], n_ + 1)
                    stageP(i, g)
                if self.sub < 5:
                    return
                mk = cf("mk")
                for s in range(2):
                    qc = slice(TP + 16 * s, TP + 16 * (s + 1))
                    for g in range(2):
                        bc = nextbank([0, 1, 2, 3])
                        bn = nextbank([0, 1, 2, 3])
                        for hh in range(4):
                            jj = 2 * g + hh // 2
                            half = hh % 2
                            mm(PS[bc][:, hh * 16:(hh + 1) * 16], KCs[:, s, g, :], QT[:, jj, half, qc], True, True, [rKCs, rQT], [rPS[bc]])
                            mm(PS[bn][:TS, hh * 16:(hh + 1) * 16], KT[:, g, 128 + TP:128 + T], QT[:, jj, half, qc], True, True, [rKT, rQT], [rPS[bn]])
                        act(PTsc[:, :, :], PS[bc][:, 0:64].rearrange("p (h q) -> p h q", h=4), AF.Exp, [rPS[bc]], [rPTsc], scale=0.125)
                        act(PTsn[:TS, :, :], PS[bn][:TS, 0:64].rearrange("p (h q) -> p h q", h=4), AF.Exp, [rPS[bn]], [rPTsn], scale=0.125)
                        ts("dve", PTsn[:TS, :, :], PTsn[:TS, :, :], mk[:TS, s:s + 1], None, ALU.mult, None, [rPTsn, rCF], [rPTsn])
                        for jl in range(2):
                            jj = 2 * g + jl
                            bo = nextbank([4, 5, 6, 7])
                            for which in range(2):
                                col = slice(which * 16, (which + 1) * 16)
                                m = 0
                                for src in range(2):
                                    for half in range(2):
                                        hh = 2 * jl + half
                                        if src == 0:
                                            lh = VCs[:, s, g, half, :] if which == 0 else (cb("ones_lo") if half == 0 else cb("ones_hi"))
                                            rhs = PTsc[:, hh, :]
                                        else:
                                            lh = VLH[:TS, 9, 2 * g + half, :] if which == 0 else (cb("ones_lo")[:TS, :] if half == 0 else cb("ones_hi")[:TS, :])
                                            rhs = PTsn[:TS, hh, :]
                                        mm(PS[bo][:, col], lh, rhs, m == 0, m == 3, [rVCs, rVLH, rPTsc, rPTsn, rCB], [rPS[bo]])
                                        m += 1
                            rd, rrd = RD[jl]
                            ts("dve", rd[:, 0:16], PS[bo][:, 16:32], ESK[:, l, jj:jj + 1], None, ALU.add, None, [rPS[bo], rESK], [rrd])
                            recip(rd[:, 0:16], rd[:, 0:16], [rrd], [rrd])
                            tt("dve", BR[:, jj, qc], PS[bo][:, 0:16], rd[:, 0:16], ALU.mult, [rPS[bo], rrd], [rBR])

            def ssm(b, l):
                areset()
                DT, rDT = aalloc([9, 16], F32)
                DTA, rDTA = aalloc([9, 16], F32)
                tmpd, rtmpd = aalloc([16], F32)
                wdt, rwdt = wnext(("in", l, "dt"))
                dtb = cf("dtb", l)
                for i in range(9):
                    rows = 128 if i < 8 else TS
                    t0 = i * 128
                    bank = nextbank([0, 1, 2, 3])
                    for k in range(8):
                        mm(PS[bank][:rows, 0:16], H[:, k, t0:t0 + rows], wdt[:, k, 0:16], k == 0, k == 7, [rwdt] + rHof(t0, rows), [rPS[bank]])
                    tt("dve", tmpd[:rows, :], PS[bank][:rows, 0:16], dtb[:rows, :], ALU.add, [rPS[bank], rCF], [rtmpd])
                    act(tmpd[:rows, :], tmpd[:rows, :], AF.Exp, [rtmpd], [rtmpd])
                    act(DT[:rows, i, :], tmpd[:rows, :], AF.Ln, [rtmpd], [rDT], bias=1.0)
                    tt("dve", DTA[:rows, i, :], DT[:rows, i, :], AROW[:rows, l, :], ALU.mult, [rDT, rAROW], [rDTA])
                DTAH, rDTAHL = aalloc([9, 16], BF16)
                DTAL, _ = aalloc([9, 16], BF16)
                dres, rdres = aalloc([9, 16], F32)
                memset("dve", DTA[:, :, :], 0.0, [rDTA]) if False else None
                for (r0, r1, tiles) in ((0, 128, slice(0, 8)), (0, TS, slice(8, 9))):
                    cpy("dve", DTAH[r0:r1, tiles, :], DTA[r0:r1, tiles, :], [rDTA], [rDTAHL])
                    tt("dve", dres[r0:r1, tiles, :], DTA[r0:r1, tiles, :], DTAH[r0:r1, tiles, :], ALU.subtract, [rDTA, rDTAHL], [rdres])
                    cpy("dve", DTAL[r0:r1, tiles, :], dres[r0:r1, tiles, :], [rdres], [rDTAHL])
                mark = arena_top[0]
                for g in range(2):
                    if g == 1:
                        P.barrier()
                        arena_top[0] = mark
                    ssm_group(b, l, g, DT, rDT, DTAH, DTAL, rDTAHL)
                P.barrier()
                arena_top[0] = mark
                rstd, rrstd = aalloc([T], F32)
                sqs = [aalloc([T], BF16) for _ in range(2)]
                banks = [0, 1, 2]
                for c in range(8):
                    sq, rsq = sqs[c % 2]
                    act(sq[:, :], BR[:, c, :], AF.Square, [rBR], [rsq])
                    for ti, (t0, tn) in enumerate(NT):
                        mm(PS[banks[ti]][:, :tn], cb("ones"), sq[:, t0:t0 + tn], c == 0, c == 7, [rsq, rCB], [rPS[banks[ti]]])
                for ti, (t0, tn) in enumerate(NT):
                    ts("dve", rstd[:, t0:t0 + tn], PS[banks[ti]][:, :tn], 1.0 / D, EPS, ALU.mult, ALU.add, [rPS[banks[ti]]], [rrstd])
                rsqrt_inplace(rstd[:, :], rrstd)
                nw = cf("normw", l)
                for c in range(8):
                    stt("dve", BR[:, c, :], BR[:, c, :], nw[:, c:c + 1], rstd[:, :], ALU.mult, ALU.mult, [rBR, rrstd, rCF], [rBR])

            def ssm_group(b, l, g, DT, rDT, DTAH, DTAL, rDTAHL):
                SZ, rSZ = aalloc([4, T], BF16)
                XC, rXC = aalloc([6, T], BF16)
                mark_c = arena_top[0]
                dsk_l = cf("dskip", l)
                PREs = [aalloc([3 + TP], F32) for _ in range(2)]
                PRSs = [aalloc([2, 19], F32) for _ in range(2)]
                ACCs = [aalloc([T], F32) for _ in range(2)]
                convw = cf("convw", l)
                convb = cf("convb", l)
                for j in range(4):
                    wz, rwz = wnext(("in", l, "z", 4 * g + j))

                    def ez(bank, ti, t0, tn, j=j):
                        act(SZ[:, j, t0:t0 + tn], PS[bank][:, :tn], AF.Silu, [rPS[bank]], [rSZ])
                    proj_fm(wz, rwz, 8, 0, H, rHof, [0, 1, 2, 3, 4, 5, 6, 7], ez)
                cpend = []
                flist = [("x", 4 * g + j, 4 * g + j) for j in range(4)] + [("B", g, 8 + g), ("C", g, 10 + g)]
                for fi, (kind, idx, f) in enumerate(flist):
                    wx, rwx = wnext(("in", l, kind, idx))
                    PRE, rPRE = PREs[fi % 2]
                    PRS, rPRS = PRSs[fi % 2]
                    ACC, rACC = ACCs[fi % 2]
                    if b == 0:
                        memset("dve", PRE[:, 0:3], 0.0, [rPRE])
                    else:
                        cpy("dve", PRE[:, 0:3], CCAR[:, f, :], [rCCAR], [rPRE])
                    for s in range(2):
                        P.dma("sp", ldsl, PRS[:, s, 0:3], cs_d[2 * b + s, l, :, f, :], writes=[rPRS])

                    def ex(bank, ti, t0, tn, PRE=PRE, rPRE=rPRE, PRS=PRS, rPRS=rPRS):
                        pe_ = min(t0 + tn, TP)
                        if pe_ > t0:
                            cpy("act", PRE[:, 3 + t0:3 + pe_], PS[bank][:, 0:pe_ - t0], [rPS[bank]], [rPRE])
                        if t0 + tn > TP:
                            assert t0 <= TP and t0 + tn == T
                            cpy("act", PRS[:, :, 3:19], PS[bank][:, TP - t0:TP - t0 + TS].rearrange("p (s t) -> p s t", s=2), [rPS[bank]], [rPRS])
                    proj_fm(wx, rwx, 8, 0, H, rHof, [0, 1, 2, 3, 4, 5, 6, 7], ex)
                    if b == 0:
                        cpy("dve", CCAR[:, f, :], PRE[:, TP:TP + 3], [rPRE], [rCCAR])
                    else:
                        odma(cp_o[l, :, f, :], PRE[:, TP:TP + 3], [rPRE])
                    for s in range(2):
                        odma(cso_o[l, 2 * b + s, :, f, :], PRS[:, s, 16:19], [rPRS])
                    wc = convw[:, f * 4:(f + 1) * 4]
                    ts("dve", ACC[:, 0:TP], PRE[:, 3:3 + TP], wc[:, 3:4], convb[:, f:f + 1], ALU.mult, ALU.add, [rPRE, rCF], [rACC])
                    accs = ACC[:, TP:T].rearrange("p (s t) -> p s t", s=2)
                    ts("dve", accs, PRS[:, :, 3:19], wc[:, 3:4], convb[:, f:f + 1], ALU.mult, ALU.add, [rPRS, rCF], [rACC])
                    for i in range(3):
                        stt("dve", ACC[:, 0:TP], PRE[:, i:i + TP], wc[:, i:i + 1], ACC[:, 0:TP], ALU.mult, ALU.add, [rPRE, rACC, rCF], [rACC])
                        stt("dve", accs, PRS[:, :, i:i + 16], wc[:, i:i + 1], accs, ALU.mult, ALU.add, [rPRS, rACC, rCF], [rACC])
                    if cpend:
                        cpend.pop()()
                    cpend.append(lambda fi=fi, ACC=ACC, rACC=rACC: act(XC[:, fi, :], ACC[:, :], AF.Silu, [rACC], [rXC]))
                if cpend:
                    cpend.pop()()

                P.barrier()
                arena_top[0] = mark_c
                NH = 8
                RENG = "dve"
                RH, rRH = aalloc([NH, 128], BF16)
                RL, rRL = aalloc([NH, 128], BF16)
                DEC, rDEC = aalloc([NH, 128], BF16)
                EROW, rEROW = aalloc([NH, 128], F32)
                CBM, rCBM = aalloc([128], F32)
                MTs = [aalloc([NH, 128], BF16) for _ in range(2)]
                CEXPs = [aalloc([NH, 128], BF16) for _ in range(2)]
                XBTs = [aalloc([5, 128], BF16) for _ in range(2)]
                XDTs = [aalloc([4, 2, 128], BF16) for _ in range(2)]
                XDWs = [[aalloc([512], BF16)] for _ in range(2)]
                ELAs = [aalloc([2, NH], F32) for _ in range(2)]
                DTW, rDTW = aalloc([NH], F32)
                BMs = [aalloc([2, 128], BF16) for _ in range(2)]
                DG, rDG = aalloc([4, 128], BF16)
                for j in range(4):
                    ts("dve", DG[:, j, :], cb("ident"), dsk_l[:, 4 * g + j:4 * g + j + 1], None, ALU.mult, None, [rCB, rCF], [rDG])
                WW, rWW = aalloc([NH], F32)
                HT, rHT = aalloc([512], F32)
                HLOS = [aalloc([4, 2, 128], BF16) for _ in range(2)]
                HS0 = [aalloc([512], F32) for _ in range(2)]
                for _h, _r in XDTs + HLOS:
                    memset("dve", _h[:, :, :, :], 0.0, [_r])
                hsl = slice(g * 8, (g + 1) * 8)
                dsk = cf("dskip", l)
                if b == 0:
                    memset("dve", HST[:, g, :], 0.0, [rHST[g]])

                def padview(buf, rows):
                    a = buf[0:rows, 0, 0, 0:64]
                    pst = a.ap[0][0]
                    return bass.AP(a.tensor, a.offset, [[pst, rows], [256, 4], [192, 2], [1, 64]])

                def pair_hdr(pi):
                    samp = pi == 8
                    rows = TS if samp else 128
                    rs = slice(0, rows)
                    t0 = pi * 128
                    tc = slice(t0, t0 + rows)
                    par_ = pi % 2
                    MT, rMT = MTs[par_]
                    CEXP, rCEXP = CEXPs[par_]
                    XBT, rXBT = XBTs[par_]
                    XDT, rXDT = XDTs[par_]
                    ELA, rELA = ELAs[par_]
                    m01 = cf("m01s")[rs, 0:rows] if samp else cf("m01")[:, :]
                    m01b = cb("m01s")[rs, 0:rows] if samp else cb("m01")[:, :]
                    sub = 16 if samp else 64
                    mcol = cf("mk") if samp else cf("mkp")
                    return locals()

                def front1(pi, part):
                    L = pair_hdr(pi)
                    samp, rows, rs, t0, tc, par_ = L['samp'], L['rows'], L['rs'], L['t0'], L['tc'], L['par_']
                    MT, rMT, CEXP, rCEXP, XBT, rXBT, XDT, rXDT, ELA, rELA = (L[k] for k in ('MT', 'rMT', 'CEXP', 'rCEXP', 'XBT', 'rXBT', 'XDT', 'rXDT', 'ELA', 'rELA'))
                    m01, m01b, sub, mcol = L['m01'], L['m01b'], L['sub'], L['mcol']
                    if part == 'b':
                        for hb in range(2):
                            hcols = slice(hb * 4, (hb + 1) * 4)
                            od = PS[hb][rs, 0:4 * rows].rearrange("p (h t) -> p h t", h=4)
                            oc = PS[2 + hb][:, 0:4 * rows].rearrange("p (h t) -> p h t", h=4)
                            act(DEC[rs, hcols, 0:rows], od, AF.Exp, [rPS[hb]], [rDEC])
                            act(EROW[:, hcols, 0:rows], oc, AF.Exp, [rPS[2 + hb]], [rEROW])
                        bcb = 4
                        mm(PS[bcb][rs, 384:384 + rows], XC[:, 4, tc], XC[:, 5, tc], True, True, [rXC], [rPS[bcb]])
                        return
                    btp = 4
                    for ci in range(5):
                        tp(PS[btp][rs, :].bitcast(BF16)[:, ci * 128:(ci + 1) * 128], XC[:, ci, tc], cb("ident"),
                           [rXC, rCB], [rPS[btp]])
                    cpy("act", XBT[rs, :, :], PS[btp][rs, :].bitcast(BF16)[:, 0:640].rearrange("p (c m) -> p c m", c=5), [rPS[btp]], [rXBT])
                    for (Rb, rRb, Dsrc) in ((RH, rRH, DTAH),):
                        tt(RENG, Rb[rs, :, 0:rows], m01b.unsqueeze(1).to_broadcast([rows, NH, rows]),
                           Dsrc[rs, pi, hsl].unsqueeze(2).to_broadcast([rows, NH, rows]), ALU.mult, [rDTAHL, rCB], [rRb])
                    for hb in range(2):
                        hcols = slice(hb * 4, (hb + 1) * 4)
                        od = PS[hb][rs, 0:4 * rows].rearrange("p (h t) -> p h t", h=4)
                        oc = PS[2 + hb][:, 0:4 * rows].rearrange("p (h t) -> p h t", h=4)
                        mm(od, cb("su")[rs, 0:rows], RH[rs, hcols, 0:rows], True, True, [rRH, rCB], [rPS[hb]])
                        mm(oc, cb("ones")[rs, :], RH[rs, hcols, 0:rows], True, True, [rRH, rCB], [rPS[2 + hb]])

                def front2(pi, part):
                    L = pair_hdr(pi)
                    samp, rows, rs, t0, tc, par_ = L['samp'], L['rows'], L['rs'], L['t0'], L['tc'], L['par_']
                    MT, rMT, CEXP, rCEXP, XBT, rXBT, XDT, rXDT, ELA, rELA = (L[k] for k in ('MT', 'rMT', 'CEXP', 'rCEXP', 'XBT', 'rXBT', 'XDT', 'rXDT', 'ELA', 'rELA'))
                    m01, m01b, sub, mcol = L['m01'], L['m01b'], L['sub'], L['mcol']
                    bcb = 4
                    if part == 'a':
                        tt("dve", CBM[rs, 0:rows], PS[bcb][rs, 384:384 + rows], m01, ALU.mult, [rPS[bcb], rCF], [rCBM])
                        tt("dve", MT[rs, :, 0:rows], DEC[rs, :, 0:rows], CBM[rs, 0:rows].unsqueeze(1).to_broadcast([rows, NH, rows]),
                           ALU.mult, [rDEC, rCBM], [rMT])
                        tt(RENG, CEXP[:, :, 0:rows], EROW[:, :, 0:rows], XC[:, 5, tc].unsqueeze(1).to_broadcast([128, NH, rows]),
                           ALU.mult, [rEROW, rXC], [rCEXP])
                        return
                    for sc in range(2):
                        cpy("act", ELA[:, sc, :], EROW[:, :, (sc + 1) * sub - 1], [rEROW], [rELA])
                    if samp:
                        ts("dve", WW[rs, :], DEC[rs, :, 15], mcol[rs, 0:1], None, ALU.mult, None, [rDEC, rCF], [rWW])
                        stt("dve", WW[rs, :], DEC[rs, :, 31], mcol[rs, 1:2], WW[rs, :], ALU.mult, ALU.add, [rDEC, rCF, rWW], [rWW])
                    else:
                        cpy("act", WW[0:64, :], DEC[0:64, :, 63], [rDEC], [rWW])
                        cpy("act", WW[64:128, :], DEC[64:128, :, 127], [rDEC], [rWW])
                    tt("dve", DTW[rs, :], WW[rs, :], DT[rs, pi, hsl], ALU.mult, [rWW, rDT], [rDTW])
                    BM, rBM = BMs[par_]
                    for sc in range(2):
                        ts("dve", BM[rs, sc, :], XBT[rs, 4, :], mcol[rs, sc:sc + 1], None, ALU.mult, None, [rXBT, rCF], [rBM])
                    xtok4 = XBT[rs, 0:4, :].rearrange("p c (v d) -> p c v d", v=2)
                    dt4 = DT[rs, pi, hsl].rearrange("p (c v) -> p c v", v=2)
                    tt("dve", padview(XDT, rows), xtok4, dt4.unsqueeze(3).to_broadcast([rows, 4, 2, 64]), ALU.mult, [rXBT, rDT], [rXDT])
                    xtok8 = XBT[rs, 0:4, :].rearrange("p c (v d) -> p (c v) d", v=2)
                    xw, rxw = XDWs[par_][0]
                    tt("dve", xw[rs, :].rearrange("p (h d) -> p h d", h=NH), xtok8,
                       DTW[rs, :].unsqueeze(2).to_broadcast([rows, NH, 64]), ALU.mult, [rXBT, rDTW], [rxw])
                    BM, rBM = BMs[par_]
                    for sc in range(2):
                        sb_ = (7, 5)[sc]
                        mm(PS[sb_][:, :], BM[rs, sc, :], xw[rs, :], True, True, [rBM, rxw], [rPS[sb_]])

                def back(pi, part):
                    L = pair_hdr(pi)
                    samp, rows, rs, t0, tc, par_ = L['samp'], L['rows'], L['rs'], L['t0'], L['tc'], L['par_']
                    MT, rMT, CEXP, rCEXP, XBT, rXBT, XDT, rXDT, ELA, rELA = (L[k] for k in ('MT', 'rMT', 'CEXP', 'rCEXP', 'XBT', 'rXBT', 'XDT', 'rXDT', 'ELA', 'rELA'))
                    m01, m01b, sub, mcol = L['m01'], L['m01b'], L['sub'], L['mcol']
                    BM, rBM = BMs[par_]
                    xw, rxw = XDWs[par_][0]
                    by = 6
                    bs_ = 7
                    for sc in (range(2) if part == 'state' else ()):
                        hlo, rhlo = HLOS[sc]
                        if samp:
                            hsrc, rhsrc = HS0[sc]
                            P.dma("sp", ldsl, hsrc[:, :], hs_d[2 * b + sc, l, :, g * 512:(g + 1) * 512], writes=[rhsrc])
                            hcur, rhcur = hsrc, rhsrc
                        else:
                            hcur, rhcur = HST[:, g, :], rHST[g]
                        cpy("act", padview(hlo, 128), hcur.rearrange("p (j v d) -> p j v d", j=4, v=2), [rhcur], [rhlo])
                        bs_ = (7, 5)[sc]
                        tt("dve", HT[:, :].rearrange("p (h d) -> p h d", h=NH), hcur.rearrange("p (h d) -> p h d", h=NH),
                           ELA[:, sc, :].unsqueeze(2).to_broadcast([128, NH, 64]), ALU.mult, [rhcur, rELA], [rHT])
                        if samp:
                            tt("dve", hcur, HT[:, :], PS[bs_][:, :], ALU.add, [rHT, rPS[bs_]], [rhcur])
                            odma(hso_o[l, 2 * b + sc, :, g * 512:(g + 1) * 512], hcur, [rhcur])
                        else:
                            tt("dve", HST[:, g, :], HT[:, :], PS[bs_][:, :], ALU.add, [rHT, rPS[bs_]], [rHST[g]])
                    for j in (range(4) if part == 'y' else ()):
                        for par in range(2):
                            mm(PS[by][:, j * rows:(j + 1) * rows], XDT[rs, j, par, :], MT[rs, 2 * j + par, 0:rows], par == 0, False,
                               [rXDT, rMT], [rPS[by]])
                        mm(PS[by][:, j * rows:(j + 1) * rows], DG[:, j, :], XC[:, j, tc], False, False, [rDG, rXC], [rPS[by]])
                        for sc in range(2):
                            hlo, rhlo = HLOS[sc]
                            for par in range(2):
                                mm(PS[by][:, j * rows + sc * sub:j * rows + (sc + 1) * sub], hlo[:, j, par, :],
                                   CEXP[:, 2 * j + par, sc * sub:(sc + 1) * sub], False, par == 1 and sc == 1, [rhlo, rCEXP], [rPS[by]])
                    if part == 'evac':
                        tt("dve", BR[:, 4 * g:4 * g + 4, tc], PS[by][:, 0:4 * rows].rearrange("p (j t) -> p j t", j=4), SZ[:, :, tc], ALU.mult,
                           [rPS[by], rSZ], [rBR])

                front1(0, 'a')
                front1(0, 'b')
                front2(0, 'a')
                front2(0, 'b')
                for pi in range(9):
                    nxt = pi + 1 < 9
                    if nxt:
                        front1(pi + 1, 'a')
                    back(pi, 'state')
                    if nxt:
                        front1(pi + 1, 'b')
                    back(pi, 'y')
                    if nxt:
                        front2(pi + 1, 'a')
                    back(pi, 'evac')
                    if nxt:
                        front2(pi + 1, 'b')
                if b == 1:
                    odma(hp_o[l, :, g * 512:(g + 1) * 512], HST[:, g, :], [rHST[g]])

            def gelu_A(src, rsrc, tmpa, rta):
                act(tmpa, src, AF.Square, [rsrc], [rta])
                ts("dve", tmpa, tmpa, 0.044715, 1.0, ALU.mult, ALU.add, [rta], [rta])
                tt("dve", tmpa, tmpa, src, ALU.mult, [rta, rsrc], [rta])

            def gelu_B(dst, src, rsrc, rdst, tmpa, rta, tmpb, rtb):
                act(tmpb, tmpa, AF.Sigmoid, [rta], [rtb], scale=2.0 * math.sqrt(2.0 / math.pi))
                tt("dve", dst, tmpb, src, ALU.mult, [rtb, rsrc], [rdst])

            def gmlp(b, l):
                areset()
                GU, rGU = aalloc([4, T], BF16)
                VT, rVT = aalloc([9, 512], BF16)
                GT, rGT = aalloc([GT_W], F32)
                GW, rGW = aalloc([GW_W], BF16)
                raws = [aalloc([512], F32) for _ in range(3)]
                tas = [aalloc([512], F32) for _ in range(2)]
                tbs = [aalloc([512], F32) for _ in range(2)]
                gcnt = [0]
                pend = []
                gls = [aalloc([512], F32) for _ in range(9)]
                vfs = [aalloc([512], F32) for _ in range(2)]
                sts = [aalloc([8], F32) for _ in range(2)]
                ags = [aalloc([4], F32) for _ in range(2)]
                MXs = [aalloc([512], F32) for _ in range(2)]
                P.dma("sp", ldsl, GT[:, :], gt_d[l], writes=[rGT])
                pool_arena_dma(GW[:, :], gw_d[l], [rGW])
                for j in range(4):
                    wu, rwu = wnext(("in", l, "gu", j))

                    def eu(bank, ti, t0, tn, j=j):
                        raw, rraw = raws[gcnt[0] % 3]
                        ta, rta = tas[gcnt[0] % 2]
                        tb_, rtb = tbs[gcnt[0] % 2]
                        gcnt[0] += 1
                        cpy("act", raw[:, :tn], PS[bank][:, :tn], [rPS[bank]], [rraw])
                        gelu_A(raw[:, :tn], rraw, ta[:, :tn], rta)
                        if pend:
                            pend.pop()()
                        pend.append(lambda: gelu_B(GU[:, j, t0:t0 + tn], raw[:, :tn], rraw, rGU, ta[:, :tn], rta, tb_[:, :tn], rtb))
                    proj_fm(wu, rwu, 8, 0, H, rHof, [0, 1, 2, 3], eu)
                if pend:
                    pend.pop()()
                wvs = [wnext(("in", l, "gv", j)) for j in range(4)]
                lng = GT[:, 0:512]
                lnb = GT[:, 512:1024]
                def gvA(i):
                    rows = 128 if i < 8 else TS
                    rs = slice(0, rows)
                    t0 = i * 128
                    bank = nextbank([0, 1, 2, 3])
                    for j in range(4):
                        wv_, rwv_ = wvs[j]
                        for k in range(8):
                            mm(PS[bank][rs, j * 128:(j + 1) * 128], H[:, k, t0:t0 + rows], wv_[:, k, :], k == 0, k == 7, [rwv_] + rHof(t0, rows), [rPS[bank]])
                    raw, rraw = raws[i % 3]
                    ta, rta = tas[i % 2]
                    cpy("act", raw[rs, :], PS[bank][rs, :], [rPS[bank]], [rraw])
                    gelu_A(raw[rs, :], rraw, ta[rs, :], rta)

                def gvB(i):
                    rows = 128 if i < 8 else TS
                    rs = slice(0, rows)
                    raw, rraw = raws[i % 3]
                    ta, rta = tas[i % 2]
                    tb_, rtb = tbs[i % 2]
                    gl, rgl = gls[i]
                    gelu_B(gl[rs, :], raw[rs, :], rraw, rgl, ta[rs, :], rta, tb_[rs, :], rtb)

                for i in range(10):
                    if i < 9:
                        gvA(i)
                    if i >= 1:
                        gvB(i - 1)

                def lnC(i):
                    rows = 128 if i < 8 else TS
                    rs = slice(0, rows)
                    gl, rgl = gls[i]
                    st, rst = sts[i % 2]
                    ag, rag = ags[i % 2]
                    P.add("dve", lambda e, rs=rs, st=st, gl=gl: e.bn_stats(out=st[rs, 0:6], in_=gl[rs, :]), reads=[rgl], writes=[rst])
                    P.add("dve", lambda e, rs=rs, st=st, ag=ag: e.bn_aggr(out=ag[rs, 0:2], in_=st[rs, 0:6]), reads=[rst], writes=[rag])
                    ts("dve", ag[rs, 2:3], ag[rs, 1:2], EPS, None, ALU.add, None, [rag], [rag])

                def lnD(i):
                    rows = 128 if i < 8 else TS
                    rs = slice(0, rows)
                    ag, rag = ags[i % 2]
                    rsqrt_inplace(ag[rs, 2:3], rag)

                def lnE(i):
                    rows = 128 if i < 8 else TS
                    rs = slice(0, rows)
                    gl, rgl = gls[i]
                    vf, rvf = vfs[i % 2]
                    ag, rag = ags[i % 2]
                    ts("dve", vf[rs, :], gl[rs, :], ag[rs, 0:1], ag[rs, 2:3], ALU.subtract, ALU.mult, [rgl, rag], [rvf])
                    tt("dve", vf[rs, :], vf[rs, :], lng[rs, :], ALU.mult, [rvf, rGT], [rvf])
                    tt("dve", vf[rs, :], vf[rs, :], lnb[rs, :], ALU.add, [rvf, rGT], [rvf])
                    cpy("act", VT[rs, i, :], vf[rs, :], [rvf], [rVT])
                    if i == 8:
                        odma(gv_o[l, b], vf[rs, :], [rvf])

                lnC(0)
                for i in range(9):
                    if i + 1 < 9:
                        lnC(i + 1)
                    lnD(i)
                    lnE(i)
                tt("dve", GW[:, 0:512].rearrange("p (g t) -> p g t", g=4), GW[:, 0:512].rearrange("p (g t) -> p g t", g=4),
                   cf("m01g")[:, :].unsqueeze(1).to_broadcast([128, 4, 128]), ALU.mult, [rGW, rCF], [rGW])
                tt("dve", GW[0:32, 512:640].rearrange("p (g t) -> p g t", g=4), GW[0:32, 512:640].rearrange("p (g t) -> p g t", g=4),
                   cf("m01s")[0:32, :].unsqueeze(1).to_broadcast([32, 4, 32]), ALU.mult, [rGW, rCF], [rGW])
                for i in range(9):
                    rows = 128 if i < 8 else TS
                    rs = slice(0, rows)
                    t0 = i * 128
                    tc = slice(t0, t0 + rows)
                    bank = nextbank([4, 5, 6, 7])
                    for gg in range(4):
                        if i < 8:
                            rhs = GW[:, gg * 128:(gg + 1) * 128]
                        else:
                            rhs = GW[0:32, 512 + gg * 32:512 + (gg + 1) * 32]
                        mm(PS[bank][:, gg * rows:(gg + 1) * rows], VT[rs, i, gg * 128:(gg + 1) * 128], rhs, True, True, [rVT, rGW], [rPS[bank]])
                    if i < 8:
                        bias = GT[:, 1024:1536]
                    else:
                        bias = GT[:, 1536:1664]
                    MX, rMX = MXs[i % 2]
                    tt("dve", MX[:, 0:4 * rows], PS[bank][:, 0:4 * rows], bias, ALU.add, [rPS[bank], rGT], [rMX])
                    tt("dve", BR[:, 0:4, tc], MX[:, 0:4 * rows].rearrange("p (g t) -> p g t", g=4), GU[:, :, tc], ALU.mult, [rMX, rGU], [rBR])

            def out_proj(b, l):
                areset()
                for c in range(8):
                    wo, rwo = wnext(("wo", l, c))

                    def eo(bank, ti, t0, tn, c=c):
                        for (g0, gn, s) in groups(b):
                            lo = max(g0, t0)
                            hi = min(g0 + gn, t0 + tn)
                            if lo >= hi:
                                continue
                            stt("dve", X[b][:, c, lo:hi], PS[bank][:, lo - t0:hi - t0], modcol(l, 2, c, s), X[b][:, c, lo:hi],
                                ALU.mult, ALU.add, [rPS[bank], rMODl[l], rX[b][c]], [rX[b][c]])
                    proj_fm(wo, rwo, 8, 0, MG, rMG, [0, 1, 2, 3, 4, 5, 6, 7], eo)

            def ffn(b, l):
                areset()
                inter = (l == 0 and b == 1)
                fbanks = [0, 1, 2, 3, 4, 5, 6] if inter else [0, 1, 2, 3, 4, 5, 6, 7]
                norm_mod(b, l, 1, 3)
                mark = arena_top[0]
                for hh in range(2):
                    P.barrier()
                    arena_top[0] = mark
                    Fh, rF = aalloc([16, T], BF16)
                    rls = [aalloc([512], F32) for _ in range(2)]
                    cnt = [0]
                    for c in range(16):
                        if inter:
                            for _ in range(2):
                                if ada_todo:
                                    ada_chunk(1, ada_todo.pop(0), 7)
                        w1, rw1 = wnext(("f1", l, hh * 16 + c))

                        def e1(bank, ti, t0, tn, c=c):
                            rl, rrl = rls[cnt[0] % 2]
                            cnt[0] += 1
                            act(rl[:, :tn], PS[bank][:, :tn], AF.Relu, [rPS[bank]], [rrl])
                            tt("dve", Fh[:, c, t0:t0 + tn], rl[:, :tn], rl[:, :tn], ALU.mult, [rrl], [rF])
                        proj_fm(w1, rw1, 8, 0, H, rHof, fbanks, e1)
                    for c in range(8):
                        w2a, rw2a = wnext(("f2", l, hh, c, 0))
                        w2b, rw2b = wnext(("f2", l, hh, c, 1))
                        for ti, (t0, tn) in enumerate(NT):
                            bank = nextbank(fbanks)
                            for k in range(16):
                                w2, rw2 = (w2a, rw2a) if k < 8 else (w2b, rw2b)
                                mm(PS[bank][:, :tn], w2[:, k % 8, :], Fh[:, k, t0:t0 + tn], k == 0, k == 15, [rw2, rF], [rPS[bank]])
                            for (g0, gn, s) in groups(b):
                                lo = max(g0, t0)
                                hi = min(g0 + gn, t0 + tn)
                                if lo >= hi:
                                    continue
                                stt("dve", X[b][:, c, lo:hi], PS[bank][:, lo - t0:hi - t0], modcol(l, 5, c, s), X[b][:, c, lo:hi],
                                    ALU.mult, ALU.add, [rPS[bank], rMODl[l], rX[b][c]], [rX[b][c]])
                if inter:
                    while ada_todo:
                        ada_chunk(1, ada_todo.pop(0), 7)
                    ada_finish(1, 7)

            def final(b):
                areset()
                rstd, rrstd = aalloc([T], F32)
                rms_rstd(b, rstd, rrstd)
                gf = cf("gfin")
                for c in range(8):
                    stt("dve", X[b][:, c, :], X[b][:, c, :], gf[:, c:c + 1], rstd[:, :], ALU.mult, ALU.mult, [rX[b][c], rrstd, rCF], [rX[b][c]])
                    odma(yT_o[b, :, c, :], X[b][:, c, :], [rX[b][c]])

            phase = [0]

            def go():
                phase[0] += 1
                return phase[0] <= self.limit

            for l in range(2):
                for b in range(2):
                    if go():
                        areset()
                        norm_mod(b, l, 0, 0)
                    if go():
                        attention(b, l)
                    if go():
                        merge_branch(l, 0, 4, True)
                    if go():
                        ssm(b, l)
                    if go():
                        merge_branch(l, 1, 8, False)
                    if go():
                        gmlp(b, l)
                    if go():
                        merge_branch(l, 2, 4, True)
                    if go():
                        out_proj(b, l)
                    if go():
                        ffn(b, l)
                    if l == 1 and go():
                        final(b)
            if self.limit < 1000:
                for b in range(2):
                    for c in range(8):
                        odma(yT_o[b, :, c, :], X[b][:, c, :], [rX[b][c]])
            lastout = [o for o in P.ops if o.slot is osl][-1]
            P.add("sp", None, deps=[lastout])
            P.emit(sems, block)
        return nc


def _chunk_from_cols(W, cols, krows=None):
    K = W.shape[0]
    if krows is None:
        krows = np.arange(8)
    out = np.zeros((128, 8, 128), np.float32)
    cols = np.asarray(cols)
    valid = cols >= 0
    for ki, k in enumerate(krows):
        blk = W[k * 128:(k + 1) * 128, :]
        out[:, ki, valid] = blk[:, cols[valid]]
    return out


def _make_chunk(key, w):
    name = key[0]
    l = key[1]
    m = np.arange(128)
    if name == "ada":
        return _chunk_from_cols(w["w_ada"][l], key[2] * 128 + m)
    if name == "in":
        W = w["w_in"][l]
        kind = key[2]
        if kind == "q":
            cols = O_Q + key[3] * 128 + m
        elif kind == "qr":
            cols = np.array([(O_Q + key[3] * 128 + (mm_ // 64) * 64 + _rot_src(mm_ % 64)) if _rot_src(mm_ % 64) >= 0 else -1 for mm_ in m])
        elif kind == "k":
            cols = O_K + key[3] * 64 + (m % 64)
        elif kind == "kr":
            cols = np.array([(O_K + key[3] * 64 + _rot_src(mm_ % 64)) if _rot_src(mm_ % 64) >= 0 else -1 for mm_ in m])
        elif kind == "v":
            cols = O_V + m
        elif kind == "z":
            cols = O_Z + key[3] * 128 + m
        elif kind == "x":
            cols = O_X + key[3] * 128 + m
        elif kind == "B":
            cols = O_B + key[3] * 128 + m
        elif kind == "C":
            cols = O_C + key[3] * 128 + m
        elif kind == "dt":
            cols = np.where(m < 16, O_DT + m, -1)
        elif kind == "gu":
            cols = O_GU + key[3] * 128 + m
        elif kind == "gv":
            cols = O_GV + key[3] * 128 + m
        elif kind == "gate":
            cols = O_G + key[3] * 1024 + key[4] * 128 + m
        else:
            raise KeyError(key)
        return _chunk_from_cols(W, cols)
    if name in ("ao", "go"):
        W = w["w_attn_o" if name == "ao" else "w_gm_o"][l]
        out = np.zeros((128, 8, 128), np.float32)
        for c2 in range(2):
            c = 2 * key[2] + c2
            for k in range(4):
                out[:, c2 * 4 + k, :] = W[k * 128:(k + 1) * 128, c * 128:(c + 1) * 128]
        return out
    if name == "so":
        return _chunk_from_cols(w["w_ssm_o"][l], key[2] * 128 + m)
    if name == "wo":
        return _chunk_from_cols(w["w_out"][l], key[2] * 128 + m)
    if name == "f1":
        return _chunk_from_cols(w["w_ff1"][l], key[2] * 128 + m)
    if name == "f2":
        hh, c, h2 = key[2], key[3], key[4]
        return _chunk_from_cols(w["w_ff2"][l], c * 128 + m, krows=hh * 16 + h2 * 8 + np.arange(8))
    raise KeyError(key)


_CACHE = {}


def _get_program():
    if "nc" not in _CACHE:
        import os
        lim = int(os.environ.get("KLIMIT", "100000"))
        bld = Builder(lim)
        n = NWCH
        nc = bld.build(n)
        assert len(bld.keys) <= n, len(bld.keys)
        while len(bld.keys) < n:
            bld.keys.append(bld.keys[0])
        _CACHE["nc"] = nc
        _CACHE["keys"] = bld.keys
    return _CACHE["nc"], _CACHE["keys"]


def _prep(inp, keys, cores=range(8)):
    f32 = np.float32
    wst = np.stack([_make_chunk(k, inp) for k in keys]).astype(f32)
    cfa = np.zeros((128, NCF), f32)
    p = np.arange(128)
    for l in range(2):
        def put(name, arr, l=l):
            o, wd = LC[name]
            cfa[:, l * LCW + o:l * LCW + o + wd] = arr
        put("gmix", inp["g_mix"][l].reshape(8, 128).T)
        put("gff", inp["g_ff"][l].reshape(8, 128).T)
        cw = inp["conv_w"][l]
        put("convw", cw.reshape(4, 12, 128).transpose(2, 1, 0).reshape(128, 48))
        put("convb", inp["conv_b"][l].reshape(12, 128).T)
        put("normw", inp["ssm_norm_w"][l].reshape(8, 128).T)
        put("dskip", np.stack([inp["d_skip"][l][2 * c + p // 64] for c in range(8)], axis=1))
        put("sink", np.stack([inp["sinks"][l][2 * j + p // 64] for j in range(4)], axis=1))
        put("bada", inp["b_ada"][l].reshape(48, 128).T)
        put("dtb", np.broadcast_to(inp["dt_bias"][l][None, :], (128, 16)))
        put("alog", np.broadcast_to(inp["a_log"][l][None, :], (128, 16)))

    def putg(name, arr):
        o, wd = GC[name]
        cfa[:, o:o + wd] = arr
    putg("gfin", inp["g_final"].reshape(8, 128).T)
    u = p[:, None]
    t = p[None, :]
    putg("m01", ((u <= t) & (u // 64 == t // 64)).astype(f32))
    putg("su", (u > t).astype(f32))
    putg("onesf", np.ones((128, 128), f32))
    m01s = np.zeros((128, 32), f32)
    uu = np.arange(32)[:, None]
    tt_ = np.arange(32)[None, :]
    m01s[:32] = ((uu <= tt_) & (uu // 16 == tt_ // 16)).astype(f32)
    putg("m01s", m01s)
    mk = np.zeros((128, 2), f32)
    mk[0:16, 0] = 1.0
    mk[16:32, 1] = 1.0
    putg("mk", mk)
    putg("m01g", (u <= t).astype(f32))
    mkp = np.zeros((128, 2), f32)
    mkp[0:64, 0] = 1.0
    mkp[64:128, 1] = 1.0
    putg("mkp", mkp)
    cba = np.zeros((128, NCB), f32)
    cba[:, BC["ident"][0]:BC["ident"][0] + 128] = np.eye(128, dtype=f32)
    cba[:, BC["ones"][0]:BC["ones"][0] + 128] = 1.0
    cba[:, BC["ones_lo"][0]:BC["ones_lo"][0] + 64] = 1.0
    cba[:, BC["ones_hi"][0] + 64:BC["ones_hi"][0] + 128] = 1.0
    cba[:, BC["su"][0]:BC["su"][0] + 128] = (u > t).astype(f32)
    cba[:, BC["m01"][0]:BC["m01"][0] + 128] = ((u <= t) & (u // 64 == t // 64)).astype(f32)
    cba[:, BC["m01s"][0]:BC["m01s"][0] + 32] = m01s
    gt = np.zeros((2, 128, GT_W), f32)
    gw = np.zeros((2, 128, GW_W), f32)
    for l in range(2):
        gt[l, :, 0:512] = inp["gm_ln_g"][l][None, :]
        gt[l, :, 512:1024] = inp["gm_ln_b"][l][None, :]
        gt[l, :, 1024:1536] = inp["gm_b_s"][l].reshape(1, 512)
        gt[l, :, 1536:1664] = np.concatenate([np.tile(inp["gm_b_s"][l][g, :16], 2) for g in range(4)])[None, :]
        for g in range(4):
            WT = inp["gm_w_s"][l][g].T
            gw[l, :, g * 128:(g + 1) * 128] = WT
            blk = np.zeros((32, 32), f32)
            blk[0:16, 0:16] = WT[0:16, 0:16]
            blk[16:32, 16:32] = WT[0:16, 0:16]
            gw[l, 0:32, 512 + g * 32:512 + (g + 1) * 32] = blk
    half = 8
    inv_freq = (500000.0 ** (-np.arange(half, dtype=np.float32) * (2.0 / 16))).astype(f32)
    rope = np.zeros((2, 2, 128, T), f32)
    for b in range(2):
        pos = np.concatenate([np.arange(b * 1024, (b + 1) * 1024), 4096 + np.arange(16), 4096 + np.arange(16)]).astype(f32)
        ang = pos[None, :] * inv_freq[:, None]
        cos = np.cos(ang).astype(f32)
        sin = np.sin(ang).astype(f32)
        for pp in range(128):
            d = pp % 64
            if d < 8:
                rope[b, 0, pp] = cos[d]
                rope[b, 1, pp] = -sin[d]
            elif d < 16:
                rope[b, 0, pp] = cos[d - 8]
                rope[b, 1, pp] = sin[d - 8]
            else:
                rope[b, 0, pp] = 1.0
    in_maps = []
    for core in cores:
        xT = np.zeros((2, 128, 8, T), f32)
        for b in range(2):
            tok = np.concatenate([inp["x_prompt"][core, b * 1024:(b + 1) * 1024],
                                  inp["x_sample"][4 * core + 2 * b], inp["x_sample"][4 * core + 2 * b + 1]], axis=0)
            xT[b] = tok.reshape(T, 8, 128).transpose(2, 1, 0)
        cs = np.concatenate([inp["c_prompt"][core][None, :], inp["c_sample"][4 * core:4 * core + 4]], axis=0)
        cT = cs.reshape(5, 8, 128).transpose(2, 1, 0)
        kc = np.zeros((4, 2, 128, 2, 128), f32)
        vc = np.zeros((4, 2, 128, 2, 2, 128), f32)
        hs = np.zeros((4, 2, 128, 1024), f32)
        cst = np.zeros((4, 2, 128, 12, 3), f32)
        for s in range(4):
            sb_ = 4 * core + s
            for l in range(2):
                ck = inp["cache_attn_k"][l, sb_]
                cv = inp["cache_attn_v"][l, sb_]
                for g in range(2):
                    kt = ck[:, g, :].T
                    kc[s, l, 0:64, g, :] = kt
                    kc[s, l, 64:128, g, :] = kt
                    vc[s, l, :, g, 0, 0:64] = cv[:, g, :]
                    vc[s, l, :, g, 1, 64:128] = cv[:, g, :]
                hs[s, l] = inp["state_ssm"][l, sb_].reshape(1024, 128).T
                cst[s, l] = inp["state_conv"][l, sb_].reshape(3, 12, 128).transpose(2, 1, 0)
        in_maps.append({"xT": np.ascontiguousarray(xT), "cT": np.ascontiguousarray(cT), "wst": wst, "cf": cfa, "cb": cba,
                        "rope": rope, "gt": gt, "gw": gw, "kc": kc, "vc": vc, "hs": hs, "cs": cst})
    return in_maps


def _assemble(R, cores=range(8)):
    f32 = np.float32
    y_prompt = np.zeros((8, 2048, 1024), f32)
    y_sample = np.zeros((32, 16, 1024), f32)
    nkp = np.zeros((2, 8, 128, 2, 64), f32)
    nvp = np.zeros((2, 8, 128, 2, 64), f32)
    nsp = np.zeros((2, 8, 16, 64, 128), f32)
    ncp = np.zeros((2, 8, 3, 1536), f32)
    nks = np.zeros((2, 32, 16, 2, 64), f32)
    nvs = np.zeros((2, 32, 16, 2, 64), f32)
    nss = np.zeros((2, 32, 16, 64, 128), f32)
    ncs = np.zeros((2, 32, 3, 1536), f32)
    ngv = np.zeros((2, 32, 16, 512), f32)
    for ci, core in enumerate(cores):
        r = R[ci]
        for b in range(2):
            tok = r["yT"][b].transpose(2, 1, 0).reshape(T, 1024)
            y_prompt[core, b * 1024:(b + 1) * 1024] = tok[:TP]
            y_sample[4 * core + 2 * b] = tok[TP:TP + 16]
            y_sample[4 * core + 2 * b + 1] = tok[TP + 16:T]
        for l in range(2):
            nkp[l, core] = r["kp"][l][0:64].transpose(2, 1, 0)
            nvp[l, core] = r["vp"][l].reshape(128, 2, 64)
            nsp[l, core] = r["hp"][l].T.reshape(16, 64, 128)
            ncp[l, core] = r["cpo"][l].transpose(2, 1, 0).reshape(3, 1536)
            for b in range(2):
                kk = r["ks"][l, b][0:64].transpose(2, 1, 0)
                vv = r["vs"][l, b].reshape(32, 2, 64)
                gg = r["gvo"][l, b]
                for s in range(2):
                    sb_ = 4 * core + 2 * b + s
                    nks[l, sb_] = kk[16 * s:16 * (s + 1)]
                    nvs[l, sb_] = vv[16 * s:16 * (s + 1)]
                    ngv[l, sb_] = gg[16 * s:16 * (s + 1)]
            for s in range(4):
                sb_ = 4 * core + s
                nss[l, sb_] = r["hso"][l, s].T.reshape(16, 64, 128)
                ncs[l, sb_] = r["cso"][l, s].transpose(2, 1, 0).reshape(3, 1536)
    return (y_prompt, y_sample, nkp, nvp, nsp, ncp, nks, nvs, nss, ncs, ngv)


def kernel(**inp):
    inp = {k: np.asarray(v) for k, v in inp.items()}
    nc, keys = _get_program()
    in_maps = _prep(inp, keys)
    res = run_bass_kernel_spmd(nc, in_maps, core_ids=list(range(8)))
    return _assemble(res.results)
```
